# Optimizing a Trainium2 kernel written in Bass

```python
import jax, jax.numpy as jnp
from jax import lax
import numpy as np

D_MODEL = 1024
BATCH = 8
SEQ = 2048
DEPTH = 2
DEC_BATCH = 128
DEC_SEQ = 1
PAST_LEN = 16384
PAGE_SIZE = 128

D_MIX = D_MODEL
W_A = D_MIX // 2
W_B = D_MIX - W_A
POOL_WINDOWS = (2, 4, 8, 16)
N_POOL_GROUPS = len(POOL_WINDOWS)
POOL_GROUP_DIM = W_A // N_POOL_GROUPS
POOL_BUF = max(POOL_WINDOWS) - 1
CHUNK = 128
N_HEADS_B = 4
HEAD_DIM_B = W_B // N_HEADS_B
D_IN = 2 * W_A + 3 * W_B
EPS = 1e-6

kernel_name = "hymba_pool_chunkmlp_step"


def rmsnorm(x, g):
    xf = x.astype(jnp.float32)
    y = xf * lax.rsqrt(jnp.mean(xf * xf, axis=-1, keepdims=True) + EPS)
    return (y * g.astype(jnp.float32)).astype(x.dtype)


def pool_mix(ctx, n_past, start_pos, w_pool, scale):
    B, L, W = ctx.shape
    T = L - n_past
    cf = ctx.astype(jnp.float32)
    cs = jnp.concatenate([jnp.zeros((B, 1, W), jnp.float32), jnp.cumsum(cf, axis=1)], axis=1)
    idx = np.arange(T) + n_past
    pos = start_pos + np.arange(T)
    means = []
    for g, w in enumerate(POOL_WINDOWS):
        sl = slice(g * POOL_GROUP_DIM, (g + 1) * POOL_GROUP_DIM)
        hi = cs[:, idx + 1, sl]
        lo = cs[:, np.maximum(idx + 1 - w, 0), sl]
        cnt = jnp.asarray(np.minimum(pos + 1, w), jnp.float32)[None, :, None]
        means.append((hi - lo) / cnt)
    d = (jnp.concatenate(means, axis=-1) - cf[:, n_past:]).astype(ctx.dtype)
    d = d.reshape(B, T, N_POOL_GROUPS, POOL_GROUP_DIM)
    y = jnp.einsum('btgc,gcd->btgd', d, w_pool).reshape(B, T, W_A)
    return y * scale


def chunk_mix(u, v, w_s, b_s):
    B, T, H, Dh = v.shape
    L = min(T, CHUNK)
    nc = -(-T // L)
    Tp = nc * L
    mask = np.tril(np.ones((L, L), dtype=bool))
    ws = jnp.where(mask[None], w_s[:, :L, :L], 0)
    vp = jnp.pad(v, ((0, 0), (0, Tp - T), (0, 0), (0, 0))).reshape(B, nc, L, H, Dh)
    s = jnp.einsum('hij,bcjhd->bcihd', ws, vp) + jnp.transpose(b_s[:, :L])[None, None, :, :, None]
    s = s.reshape(B, Tp, H, Dh)[:, :T]
    return u * s


def layer(x, past_pool, start_pos, norm_g, w_in, w_pool, pool_scale, v_norm_g, w_s, b_s, w_out):
    B, T, _ = x.shape
    h = rmsnorm(x, norm_g)
    proj = jnp.einsum('btd,de->bte', h, w_in)
    a_in = proj[..., :W_A]
    a_gate = proj[..., W_A:2 * W_A]
    u = proj[..., 2 * W_A:2 * W_A + W_B]
    v = proj[..., 2 * W_A + W_B:2 * W_A + 2 * W_B]
    b_gate = proj[..., 2 * W_A + 2 * W_B:]
    ctx = jnp.concatenate([past_pool.astype(a_in.dtype), a_in], axis=1)
    a_out = pool_mix(ctx, past_pool.shape[1], start_pos, w_pool, pool_scale) * jax.nn.silu(a_gate)
    vn = rmsnorm(v.reshape(B, T, N_HEADS_B, HEAD_DIM_B), v_norm_g.reshape(N_HEADS_B, HEAD_DIM_B))
    b_out = chunk_mix(u.reshape(B, T, N_HEADS_B, HEAD_DIM_B), vn, w_s, b_s).reshape(B, T, W_B)
    b_out = b_out * jax.nn.silu(b_gate)
    y = x + jnp.einsum('bte,ed->btd', jnp.concatenate([a_out, b_out], axis=-1), w_out)
    new_pool = ctx[:, -POOL_BUF:]
    n_last = (T - 1) % CHUNK + 1
    new_v = vn.reshape(B, T, W_B)[:, T - n_last:]
    return y, new_pool, new_v


def setup_inputs(seed: int = 0) -> dict:
    key = jax.random.key(seed)
    ks = jax.random.split(key, 12)
    nrm = jax.random.normal
    return {
        "x_prompt": nrm(ks[0], (BATCH, SEQ, D_MODEL), jnp.float32),
        "x_sample": nrm(ks[1], (DEC_BATCH, DEC_SEQ, D_MODEL), jnp.float32),
        "state_pool": nrm(ks[2], (DEPTH, DEC_BATCH, POOL_BUF, W_A), jnp.float32),
        "norm_g": 1.0 + 0.1 * nrm(ks[3], (DEPTH, D_MODEL), jnp.float32),
        "w_in": nrm(ks[4], (DEPTH, D_MODEL, D_IN), jnp.float32) * D_MODEL ** -0.5,
        "w_pool": nrm(ks[5], (DEPTH, N_POOL_GROUPS, POOL_GROUP_DIM, POOL_GROUP_DIM), jnp.float32) * POOL_GROUP_DIM ** -0.5,
        "pool_scale": 1.0 + 0.1 * nrm(ks[6], (DEPTH, W_A), jnp.float32),
        "v_norm_g": 1.0 + 0.1 * nrm(ks[7], (DEPTH, W_B), jnp.float32),
        "w_s": nrm(ks[8], (DEPTH, N_HEADS_B, CHUNK, CHUNK), jnp.float32) * CHUNK ** -0.5,
        "b_s": 1.0 + 0.1 * nrm(ks[9], (DEPTH, N_HEADS_B, CHUNK), jnp.float32),
        "w_out": nrm(ks[10], (DEPTH, D_MIX, D_MODEL), jnp.float32) * D_MIX ** -0.5,
        "final_norm_g": 1.0 + 0.1 * nrm(ks[11], (D_MODEL,), jnp.float32),
    }


def reference(x_prompt, x_sample, state_pool, norm_g, w_in, w_pool, pool_scale, v_norm_g, w_s, b_s, w_out, final_norm_g):
    yp, ys = x_prompt, x_sample
    empty_past = jnp.zeros((x_prompt.shape[0], 0, W_A), x_prompt.dtype)
    pool_p, pool_s, cv_p, cv_s = [], [], [], []
    for l in range(DEPTH):
        params = (norm_g[l], w_in[l], w_pool[l], pool_scale[l], v_norm_g[l], w_s[l], b_s[l], w_out[l])
        yp, npp, nvp = layer(yp, empty_past, 0, *params)
        ys, nps, nvs = layer(ys, state_pool[l], PAST_LEN, *params)
        pool_p.append(npp); pool_s.append(nps); cv_p.append(nvp); cv_s.append(nvs)
    y_prompt = rmsnorm(yp, final_norm_g)
    y_sample = rmsnorm(ys, final_norm_g)
    pool_prompt = jnp.stack(pool_p, axis=0)
    pool_sample = jnp.stack(pool_s, axis=0)
    chunk_v_prompt = jnp.stack(cv_p, axis=0)
    chunk_v_sample = jnp.stack(cv_s, axis=0)
    return (y_prompt, y_sample, pool_prompt, pool_sample, chunk_v_prompt, chunk_v_sample)
```

```python
import numpy as np
from contextlib import ExitStack
import concourse.bass as bass
import concourse.mybir as mybir
from concourse.bass_utils import run_bass_kernel_spmd

F32 = mybir.dt.float32
BF16 = mybir.dt.bfloat16
AF = mybir.ActivationFunctionType
ALU = mybir.AluOpType
AX = mybir.AxisListType

PE, ACT, DVE, POOL, SP = "tensor", "scalar", "vector", "gpsimd", "sync"
ENGS = (PE, ACT, DVE, POOL, SP)

NCORES = 8
D = 1024
SEQ = 2048
WA = 512
WB = 512
DIN = 2560
NS = 16
PB = 15
EPS = 1e-6
TPG = 4
GT = TPG * 128
NG = SEQ // GT
NSLOT = 8
NT = SEQ // 128
NHB = 3
HALO = 16
AW = HALO + GT
C_AIN, C_AGATE, C_U, C_V, C_BGATE = 0, 512, 1024, 1536, 2048
WINDOWS = (2, 4, 8, 16)


def _flat(xs):
    out = []
    for x in xs:
        if isinstance(x, (list, tuple)):
            out.extend(_flat(x))
        else:
            out.append(x)
    return out


class Buf:
    __slots__ = ("name", "writer", "readers")

    def __init__(self, name):
        self.name = name
        self.writer = None
        self.readers = []


class Op:
    __slots__ = ("eng", "fn", "deps", "dma", "tok", "used", "group")

    def __init__(self, eng, fn, dma=None, group=False):
        self.eng = eng
        self.fn = fn
        self.deps = []
        self.dma = dma
        self.group = group
        self.tok = None
        self.used = False


class Sched:
    def __init__(self):
        self.ops = {e: [] for e in ENGS}
        self.all = []

    def emit(self, eng, fn, reads=(), writes=(), deps=(), dma=None, group=False):
        op = Op(eng, fn, dma, group)
        reads = _flat(reads)
        writes = _flat(writes)
        dd = []
        for b in reads:
            if b.writer is not None:
                dd.append(b.writer)
        for b in writes:
            dd.extend(b.readers)
            if b.writer is not None:
                dd.append(b.writer)
        dd.extend(d for d in deps if d is not None)
        seen = set()
        for d in dd:
            if d is op or id(d) in seen:
                continue
            if d.eng == PE and eng == PE and d.dma is None:
                continue
            seen.add(id(d))
            d.used = True
            op.deps.append(d)
        for b in reads:
            b.readers.append(op)
        for b in writes:
            b.writer = op
            b.readers = []
        self.ops[eng].append(op)
        self.all.append(op)
        return op

    def resolve(self):
        keys = {}
        dma_cnt = {}
        group_ops = {}
        for e in ENGS:
            cnt = 0
            for op in self.ops[e]:
                if op.dma is not None:
                    dma_cnt[op.dma] = dma_cnt.get(op.dma, 0) + 16
                    op.tok = (op.dma, dma_cnt[op.dma])
                    keys[op.dma] = None
                    if op.group:
                        group_ops.setdefault(op.dma, []).append(op)
                elif op.used:
                    cnt += 1
                    op.tok = (e, cnt)
                    keys[e] = None
        for k, ops in group_ops.items():
            for op in ops:
                op.tok = (k, dma_cnt[k])
        return keys

    def run_engine(self, e, h, sems):
        waited = {}
        for op in self.ops[e]:
            for d in op.deps:
                key, val = d.tok
                if waited.get(key, 0) < val:
                    h.wait_ge(sems[key], val)
                    waited[key] = val
            ins = op.fn(h)
            if op.dma is not None:
                ins.then_inc(sems[op.dma], 16)
            elif op.used:
                ins.then_inc(sems[e], 1)


class Banks:
    def __init__(self, tensors, tag):
        self.t = tensors
        self.bufs = [Buf(f"{tag}{i}") for i in range(len(tensors))]
        self.nxt = 0

    def alloc(self):
        i = self.nxt
        self.nxt = (self.nxt + 1) % len(self.t)
        return self.t[i], self.bufs[i]


def build_program():
    nc = bass.Bass("TRN2", target_bir_lowering=False)
    S = Sched()

    def din(name, shape):
        return nc.dram_tensor(name, list(shape), F32, kind="ExternalInput").ap()

    def dout(name, shape):
        return nc.dram_tensor(name, list(shape), F32, kind="ExternalOutput").ap()

    x_d = din("x", [SEQ, D])
    xs_d = din("xs", [NS, D])
    st_d = din("st", [2, NS, PB, WA])
    ng_d = din("norm_g", [2, D])
    win_d = din("w_in", [2, D, DIN])
    wpool_d = din("w_pool", [2, 4, 128, 128])
    psc_d = din("pool_scale", [2, WA])
    vng_d = din("v_norm_g", [2, WB])
    ws_d = din("w_s", [2, 4, 128, 128])
    bs_d = din("b_s", [2, 4, 128])
    wout_d = din("w_out", [2, D, D])
    fng_d = din("fng", [1, D])
    cid_d = din("c_ident", [128, 128])
    ctril_d = din("c_tril", [128, 128])
    cinv_d = din("c_invcnt", [1, 64])
    csel_d = din("c_sel", [120, 128])
    cm01_d = din("c_m01", [2, 2])
    gT_d = din("gT_h", [128, 16])
    pscT_d = din("pscT_h", [128, 8])
    ws00_d = din("ws00_h", [1, 8])
    bs0_d = din("bs0_h", [1, 8])

    y_d = dout("y", [SEQ, D])
    ys_d = dout("ys", [NS, D])
    pp_d = dout("pp", [2, PB, WA])
    pso_d = dout("pso", [2, NS, PB, WA])
    cvp_d = dout("cvp", [2, 128, WB])
    cvs_d = dout("cvs", [2, NS, WB])

    with ExitStack() as es:
        def sb(name, shape, dt):
            return es.enter_context(nc.sbuf_tensor(name, list(shape), dt))

        def ps(name, shape, dt):
            return es.enter_context(nc.psum_tensor(name, list(shape), dt))

        win = [sb(f"win{l}", [128, 8, DIN], BF16) for l in range(2)]
        wout = [sb(f"wout{l}", [128, 8, D], BF16) for l in range(2)]
        wpool = sb("wpool", [128, 2, 4, 128], BF16)
        wsT = sb("wsT", [128, 2, 4, 128], BF16)
        xg = sb("xg", [128, NSLOT, D], F32)
        hb = sb("hb", [128, NHB, D], BF16)
        hT = sb("hT", [128, 8, GT], BF16)
        a_sb = sb("a_sb", [128, 2, AW], F32)
        halo = sb("halo", [128, 2, 4, HALO], F32)
        Tt = sb("Tt", [128, 2, AW], F32)
        d_sb = sb("d_sb", [128, 4, GT], BF16)
        sg = sb("sg", [128, 4, GT], BF16)
        mix = sb("mix", [128, 8, GT], BF16)
        vn_sb = sb("vn_sb", [128, TPG, WB], BF16)
        vscr = sb("vscr", [128, WB], F32)
        junk = vscr[:, :].bitcast(BF16)
        vnf = sb("vnf", [128, WB], F32)
        pp_sb_ = vnf
        pp_sb = pp_sb_
        g_bc = sb("g_bc", [128, D], F32)
        gT = sb("gT", [128, 2, 8], F32)
        vg_bc = sb("vg_bc", [128, 2, WB], F32)
        psc = sb("psc", [128, 2, 4, 1], F32)
        ident = sb("ident", [128, 128], BF16)
        identf = xg[:, 5, 512:640]
        tril = xg[:, 5, 640:768]
        wsm = xg[:, 5, 768:1024].bitcast(BF16).rearrange("p (h j) -> p h j", h=4)
        invcnt = sb("invcnt", [128, 4, 16], F32)
        fix_t = sb("fix_t", [128, 16], F32)
        nhalf = sb("nhalf", [128, 4], F32)
        epsb = sb("epsb", [128, 1], F32)
        ones2 = sb("ones2", [2, 128], BF16)
        fl = lambda ap: ap.rearrange("p a b -> p (a b)")
        r3 = lambda ap: ap.rearrange("p (h i) -> p h i", h=4)
        bsf_l = [r3(xg[0:2, 4, 0:512]), r3(xg[0:2, 4, 512:1024])]
        bhif = r3(Tt[0:2, 0, 0:512])
        bhi = r3(hb[0:2, 2, 0:512])
        blo = r3(hb[0:2, 2, 512:1024])
        bias2 = sb("bias2", [2, 2, 4, 128], BF16)
        m01 = sb("m01", [2, 2], F32)
        ssx = sb("ssx", [128, 4 * 2 * SEQ // 128 // 2 + 64], F32)
        rsx = ssx
        ss4 = sb("ss4", [128, 2 * (SEQ // 128) + 4, 4], F32)
        rs4 = ss4
        xs_sb = xg[0:NS, 0, :]
        hbs = hb[0:NS, 0, :]
        hTs = sb("hTs", [128, 8, NS], BF16)
        st_sb = xg[0:120, 1, :].rearrange("p (t c) -> p t c", t=2)
        sel = sb("sel", [120, 2, 4, NS], F32)
        ains = pp_sb_[0:NS, :]
        ds_s = d_sb[0:NS, 0, :]
        dsT = sb("dsT", [128, 4, NS], BF16)
        tmp_s = Tt[0:NS, 0, 0:512]
        sga_s = a_sb[0:NS, 0, 0:512]
        sgb_s = a_sb[0:NS, 1, 0:512]
        s_s = Tt[0:NS, 1, 0:512]
        vns = vnf[0:NS, :]
        mixs = fl(mix[0:NS, 0:2, :])
        mixTs = hTs
        psc_bc = vscr[0:NS, :]
        ws00 = sb("ws00", [NS, 2, 4, 1], F32)
        bs0 = sb("bs0", [NS, 2, 4, 1], F32)

        pT = [ps(f"pT{i}", [128, D], BF16) for i in range(2)]
        pF = [ps(f"pF{i}", [128, 512], F32) for i in range(6)]
        bankT = Banks(pT, "pT")
        bankF = Banks(pF, "pF")

        B = {}

        ALIAS = {
            "bhif": ["T0"], "bhi": ["hb2"], "blo": ["hb2"], "mixTs": ["hTs"], "pp_sb": ["vnf"],
            "hbs": ["hb0"], "st_sb0": ["x1"], "st_sb1": ["x1"], "ains": ["pp_sb"], "ds_s": ["d0"],
            "tmp_s": ["T0"], "sga_s": ["a_sb0", "a_halo0"], "sgb_s": ["a_sb1", "a_halo1"], "s_s": ["T1"],
            "vns": ["vnf"], "mixs": ["mix0", "mix1"], "xs": ["x0"], "psc_bc": ["vscr"],
        }

        def buf(name):
            if name in ALIAS:
                return [buf(n) for n in ALIAS[name]]
            if name not in B:
                B[name] = Buf(name)
            return B[name]

        col_ctr = [0]

        def newcol():
            c = col_ctr[0]
            col_ctr[0] += 1
            return c

        okey_ctr = [0]

        def _okey():
            okey_ctr[0] += 1
            return f"out{okey_ctr[0]}"

        col4_ctr = [0]

        def newcol4():
            c = col4_ctr[0]
            col4_ctr[0] += 1
            return c

        stage_ops = []

        def cdma(out_ap, in_ap, wbuf, eng=SP):
            return S.emit(eng, lambda h, o=out_ap, i=in_ap: h.dma_start(out=o, in_=i),
                          writes=[wbuf], dma="const", group=True)

        def load_tile(t, deps=()):
            slot = t % NSLOT
            r0 = t * 128
            return S.emit(SP, lambda h, s=slot, r=r0: h.dma_start(out=xg[:, s, :], in_=x_d[r:r + 128, :]),
                          writes=[buf(f"x{slot}")], dma=f"xl{slot}", deps=deps)

        S.emit(SP, lambda h: h.dma_start(out=identf[:, :], in_=cid_d[:, :]), writes=[buf("identf")], dma="cident")
        for l in range(2):
            S.emit(SP, lambda h, l=l: h.dma_start(out=gT[:, l, :], in_=gT_d[:, l * 8:(l + 1) * 8]),
                   writes=[buf(f"gT{l}")], dma=f"cgT{l}")
        for t in range(4):
            load_tile(t)
        def prep_dma(l):
            stage_ops.append(S.emit(SP, lambda h, l=l: h.dma_start(
                out=bsf_l[l][:, :, :], in_=bs_d[l].rearrange("(o h) i -> o h i", o=1).broadcast_to([2, 4, 128])),
                writes=[buf(f"bsf{l}")], dma=f"bsf{l}"))

        ckey = [0]

        def cdma1(out_ap, in_ap, wbuf):
            ckey[0] += 1
            return S.emit(SP, lambda h, o=out_ap, i=in_ap: h.dma_start(out=o, in_=i), writes=[wbuf], dma=f"c{ckey[0]}")

        cdma1(vg_bc[:, 0, :], vng_d[0:1, :].broadcast_to([128, WB]), buf("vg_bc0"))
        cdma1(tril[:, :], ctril_d[:, :], buf("tril"))
        stage_ops.append(cdma1(xg[:, 5, 0:512].rearrange("p (h j) -> p h j", h=4), ws_d[0].rearrange("h i j -> i h j"),
                               buf("wsf0")))
        stage_ops.append(cdma1(xg[:, 7, 0:512].rearrange("p (h j) -> p h j", h=4), ws_d[1].rearrange("h i j -> i h j"),
                               buf("wsf1")))
        cdma1(m01[:, :], cm01_d[:, :], buf("m01"))
        prep_dma(0)
        prep_dma(1)
        cdma1(invcnt[:, :, :].rearrange("p g t -> p (g t)"), cinv_d[0:1, :].broadcast_to([128, 64]), buf("invcnt"))
        cdma1(psc[:, 0, :, :].rearrange("p g o -> p (g o)"), pscT_d[:, 0:4], buf("psc0"))
        cdma1(vg_bc[:, 1, :], vng_d[1:2, :].broadcast_to([128, WB]), buf("vg_bc1"))
        cdma1(psc[:, 1, :, :].rearrange("p g o -> p (g o)"), pscT_d[:, 4:8], buf("psc1"))
        cdma1(g_bc[:, :], fng_d[0:1, :].broadcast_to([128, D]), buf("g_bc2"))
        cdma(sel[:, :, :, :].rearrange("p t g b -> p (t g b)"), csel_d[:, :], buf("sel"))
        for l in range(2):
            cdma(ws00[:, l, :, :].rearrange("p g o -> p (g o)"), ws00_d[0:1, l * 4:(l + 1) * 4].broadcast_to([NS, 4]),
                 buf(f"ws00{l}"))
            cdma(bs0[:, l, :, :].rearrange("p g o -> p (g o)"), bs0_d[0:1, l * 4:(l + 1) * 4].broadcast_to([NS, 4]),
                 buf(f"bs0{l}"))
        def wdma(out_ap, in_ap, wbuf, key):
            return S.emit(POOL, lambda h, o=out_ap, i=in_ap: h.dma_start(out=o, in_=i), writes=[wbuf], dma=key)

        WIN_ORDER = (C_V, C_AIN, C_AGATE, C_BGATE, C_U)
        fold_list = []
        wlist = []
        for l in range(2):
            for cb in WIN_ORDER:
                wlist.append((win[l][:, :, cb:cb + 512],
                              win_d[l][:, cb:cb + 512].rearrange("(k p) e -> p k e", p=128),
                              f"win{l}_{cb}"))
            wlist.append((wpool[:, l, :, :], wpool_d[l].rearrange("g c d -> c g d"), f"wpool{l}"))
            for hf in range(2):
                wlist.append((wout[l][:, :, hf * 512:(hf + 1) * 512],
                              wout_d[l][:, hf * 512:(hf + 1) * 512].rearrange("(k p) e -> p k e", p=128),
                              f"wout{l}_{hf}"))

        def issue_weights(n):
            for _ in range(n):
                if wlist:
                    o, i, key = wlist.pop(0)
                    wdma(o, i, buf(key), key)

        issue_weights(100)

        def fold_g(l, cb):
            def f(h):
                r = None
                for k in range(8):
                    r = h.tensor_scalar(out=win[l][:, k, cb:cb + 512], in0=win[l][:, k, cb:cb + 512],
                                        scalar1=gT[:, l, k:k + 1], scalar2=None, op0=ALU.mult)
                return r
            S.emit(DVE, f, reads=[buf(f"win{l}_{cb}"), buf(f"gT{l}")], writes=[buf(f"win{l}_{cb}")])

        S.emit(DVE, lambda h: h.memset(ones2[:, :], 1.0), writes=[buf("ones2")])
        S.emit(DVE, lambda h: h.memset(ssx[:, :], 0.0), writes=[buf("ssx")])
        S.emit(DVE, lambda h: h.memset(ss4[:, :, :], 0.0), writes=[buf("ss4")])
        S.emit(POOL, lambda h: h.memset(halo[:, :, :, :], 0.0), writes=[buf(f"halo{l}_{g}") for l in range(2) for g in range(4)])
        S.emit(DVE, lambda h: h.memset(nhalf[:, :], -0.5), writes=[buf("nhalf")])
        S.emit(DVE, lambda h: h.memset(epsb[:, :], EPS), writes=[buf("epsb")])
        ident_cast_op = S.emit(DVE, lambda h: h.tensor_copy(out=ident[:, :], in_=identf[:, :]), reads=[buf("identf")],
                               writes=[buf("ident")])

        def prep_ws(l, part):
            bsf = bsf_l[l]
            btmp = bsf
            if l == 0:
                wsf_v = xg[:, 5, 0:512].rearrange("p (h j) -> p h j", h=4)
                wsfb = buf("wsf0")
            else:
                wsf_v = xg[:, 7, 0:512].rearrange("p (h j) -> p h j", h=4)
                wsfb = buf("wsf1")
            wsm_v = wsm
            e = lambda *a, **k: stage_ops.append(S.emit(*a, **k))
            if part == "bias":
                return prep_bias(l, bsf, btmp, e)
            e(DVE, lambda h, o=wsm_v, i=wsf_v: h.tensor_tensor(
                out=o[:, :, :], in0=i, in1=tril[:, :].unsqueeze(1).broadcast_to([128, 4, 128]), op=ALU.mult),
                reads=[wsfb, buf("tril")], writes=[buf("wsm")])
            pt, ptb = bankT.alloc()

            def f_wsT(h, pt=pt, wsm_v=wsm_v):
                r = None
                for hh in range(4):
                    r = h.transpose(out=pt[:, hh * 128:(hh + 1) * 128], in_=wsm_v[:, hh, :], identity=ident[:, :])
                return r
            e(PE, f_wsT, reads=[buf("wsm"), buf("ident")], writes=[ptb])
            e(DVE, lambda h, pt=pt, l=l: h.tensor_copy(
                out=wsT[:, l, :, :].rearrange("p h i -> p (h i)"), in_=pt[:, 0:512]),
                reads=[ptb], writes=[buf(f"wsT{l}")])

        def prep_bias(l, bsf, btmp, e):
            bb = buf(f"bsf{l}")
            e(DVE, lambda h: h.tensor_copy(out=bhi[:, :, :], in_=bsf[:, :, :]), reads=[bb], writes=[buf("bhi")])
            e(DVE, lambda h: h.tensor_copy(out=bhif[:, :, :], in_=bhi[:, :, :]), reads=[buf("bhi")], writes=[buf("bhif")])
            e(DVE, lambda h: h.tensor_tensor(out=btmp[:, :, :], in0=bsf[:, :, :], in1=bhif[:, :, :], op=ALU.subtract),
              reads=[bb, buf("bhif")], writes=[bb])
            e(DVE, lambda h: h.tensor_copy(out=blo[:, :, :], in_=btmp[:, :, :]), reads=[bb], writes=[buf("blo")])
            e(DVE, lambda h: h.tensor_scalar(out=bhif[:, :, :], in0=bhi[:, :, :], scalar1=m01[:, 0:1], scalar2=None,
                                             op0=ALU.mult),
              reads=[buf("bhi"), buf("m01"), buf("blo")], writes=[buf("bhif")])
            e(DVE, lambda h, l=l: h.scalar_tensor_tensor(out=bias2[:, l, :, :], in0=blo[:, :, :], scalar=m01[:, 1:2],
                                                         in1=bhif[:, :, :], op0=ALU.mult, op1=ALU.add),
              reads=[buf("blo"), buf("bhif"), buf("m01")], writes=[buf(f"bias2{l}")])


        def xslot(G, i):
            return (G * TPG + i) % NSLOT

        def xbuf(G, i):
            return buf(f"x{xslot(G, i)}")

        def xap(G, i):
            return xg[:, xslot(G, i), :]

        folded = set()

        def ensure_fold(l, cb, only=False):
            order = list(WIN_ORDER)
            idx = order.index(cb)
            for c2 in order[idx:idx + (1 if only else 2)]:
                if (l, c2) not in folded:
                    folded.add((l, c2))
                    fold_g(l, c2)

        def rstd_ops(src_ap, src_bufs, n_feat, nparts=128, no_pool=False):
            c = newcol()
            cb_ = buf(f"ssx_{c}")
            S.emit(ACT, lambda h: h.activation(out=junk[0:nparts, 0:n_feat], in_=src_ap, func=AF.Square,
                                               accum_out=ssx[0:nparts, c:c + 1]),
                   reads=list(src_bufs) + [buf("ssx")], writes=[cb_, buf("vscr")])
            rb = buf(f"rsx_{c}")
            if no_pool:
                S.emit(ACT, lambda h: h.activation(out=rsx[0:nparts, c:c + 1], in_=ssx[0:nparts, c:c + 1], func=AF.Sqrt,
                                                   scale=1.0 / n_feat, bias=epsb[0:nparts, 0:1]),
                       reads=[cb_, buf("epsb")], writes=[rb])
                S.emit(DVE, lambda h: h.reciprocal(out=rsx[0:nparts, c:c + 1], in_=rsx[0:nparts, c:c + 1]),
                       reads=[rb], writes=[rb])
                return c, rb
            S.emit(DVE, lambda h: h.tensor_scalar(out=rsx[0:nparts, c:c + 1], in0=ssx[0:nparts, c:c + 1],
                                                  scalar1=1.0 / n_feat, scalar2=EPS, op0=ALU.mult, op1=ALU.add),
                   reads=[cb_], writes=[rb])
            S.emit(POOL, lambda h: h.tensor_tensor(out=rsx[0:nparts, c:c + 1], in0=rsx[0:nparts, c:c + 1],
                                                   in1=nhalf[0:nparts, 0:1], op=ALU.pow),
                   reads=[rb, buf("nhalf")], writes=[rb])
            return c, rb

        def norm_pre(G, l, i):
            xa = xap(G, i)
            c, rb = rstd_ops(xa, [xbuf(G, i)], D, no_pool=(G == 0 and l == 0))
            j = i % NHB
            S.emit(ACT, lambda h: h.activation(out=hb[:, j, :], in_=xa, func=AF.Copy, scale=rsx[:, c:c + 1]),
                   reads=[xbuf(G, i), rb], writes=[buf(f"hb{j}")])

        def norm_tr(G, l, i):
            j = i % NHB
            pt, ptb = bankT.alloc()

            def f_tr(h):
                r = None
                for k in range(8):
                    r = h.transpose(out=pt[:, k * 128:(k + 1) * 128], in_=hb[:, j, k * 128:(k + 1) * 128],
                                    identity=ident[:, :])
                return r
            S.emit(PE, f_tr, reads=[buf(f"hb{j}"), buf("ident")], writes=[ptb])
            S.emit(DVE, lambda h: h.tensor_copy(out=hT[:, :, i * 128:(i + 1) * 128],
                                                in_=pt[:, :].rearrange("p (k t) -> p k t", k=8)),
                   reads=[ptb], writes=[buf(f"hT{i}")])

        def hT_bufs():
            return [buf(f"hT{i}") for i in range(TPG)]

        def proj_fm(l, col0):
            cb = (col0 // 512) * 512
            ensure_fold(l, cb)
            pf, pfb = bankF.alloc()

            def f(h):
                r = None
                for k in range(8):
                    r = h.matmul(out=pf[:, 0:GT], lhsT=win[l][:, k, col0:col0 + 128], rhs=hT[:, k, :],
                                 start=(k == 0), stop=(k == 7))
                return r
            S.emit(PE, f, reads=hT_bufs() + [buf(f"win{l}_{cb}")], writes=[pfb])
            return pf, pfb

        def proj_tm(l, i, col0):
            ensure_fold(l, col0)
            pf, pfb = bankF.alloc()

            def f(h):
                r = None
                for k in range(8):
                    r = h.matmul(out=pf[:, :], lhsT=hT[:, k, i * 128:(i + 1) * 128], rhs=win[l][:, k, col0:col0 + 512],
                                 start=(k == 0), stop=(k == 7))
                return r
            S.emit(PE, f, reads=[buf(f"hT{i}"), buf(f"win{l}_{col0}")], writes=[pfb])
            return pf, pfb

        def pooling(G, l, g):
            w = WINDOWS[g]
            sl = g % 2
            ab = buf(f"a_sb{sl}")
            ahb = buf(f"a_halo{sl}")
            a = a_sb[:, sl, :]
            PENG = DVE if g >= 2 else POOL
            S.emit(PENG, lambda h: h.tensor_copy(out=a_sb[:, sl, 0:HALO], in_=halo[:, l, g, :]),
                   reads=[buf(f"halo{l}_{g}")], writes=[ahb])
            T0, T1 = Tt[:, 0, :], Tt[:, 1, :]
            tb = [buf("T0"), buf("T1")]
            steps = g + 1
            src, srcb = a, [ab, ahb]
            shift = 1
            c_from = HALO - (w - 1) + 1
            for s_ in range(steps):
                dst = (T0, T1)[s_ % 2]
                dstb = tb[s_ % 2]
                S.emit(PENG, lambda h, dst=dst, src=src, c0=c_from, sh=shift: h.tensor_tensor(
                    out=dst[:, c0:AW], in0=src[:, c0:AW], in1=src[:, c0 - sh:AW - sh], op=ALU.add),
                    reads=srcb, writes=[dstb])
                src, srcb = dst, [dstb]
                shift *= 2
                c_from += shift
            assert c_from - shift == HALO, (g, c_from, shift)
            S.emit(PENG, lambda h: h.tensor_copy(out=halo[:, l, g, :], in_=a_sb[:, sl, GT:AW]),
                   reads=[ab], writes=[buf(f"halo{l}_{g}")])
            S.emit(DVE, lambda h, src=src: h.scalar_tensor_tensor(
                out=d_sb[:, g, :], in0=src[:, HALO:AW], scalar=1.0 / w, in1=a_sb[:, sl, HALO:AW],
                op0=ALU.mult, op1=ALU.subtract),
                reads=srcb + [ab], writes=[buf(f"d{g}")])
            if G == 0:
                S.emit(DVE, lambda h, src=src: h.tensor_tensor(out=fix_t[:, :], in0=src[:, HALO:HALO + 16],
                                                               in1=invcnt[:, g, :], op=ALU.mult),
                       reads=srcb + [buf("invcnt")], writes=[buf("fix_t")])
                S.emit(DVE, lambda h: h.tensor_tensor(out=d_sb[:, g, 0:16], in0=fix_t[:, :], in1=a_sb[:, sl, HALO:HALO + 16],
                                                      op=ALU.subtract),
                       reads=[buf("fix_t"), ab], writes=[buf(f"d{g}")])

        def v_tile(G, l, i, pf, pfb, last):
            S.emit(ACT, lambda h: h.activation(out=vscr[:, :], in_=pf[:, :], func=AF.Square),
                   reads=[pfb], writes=[buf("vscr")])
            c = newcol4()
            sb_ = buf(f"ss4_{c}")
            S.emit(DVE, lambda h: h.tensor_reduce(out=ss4[:, c, :], in_=vscr[:, :].rearrange("p (h d) -> p h d", h=4),
                                                  axis=AX.X, op=ALU.add),
                   reads=[buf("vscr"), buf("ss4")], writes=[sb_])
            rb = buf(f"rs4_{c}")
            S.emit(DVE, lambda h: h.tensor_scalar(out=rs4[:, c, :], in0=ss4[:, c, :], scalar1=1.0 / 128, scalar2=EPS,
                                                  op0=ALU.mult, op1=ALU.add), reads=[sb_], writes=[rb])
            S.emit(POOL, lambda h: h.tensor_tensor(out=rs4[:, c, :], in0=rs4[:, c, :], in1=nhalf[:, :], op=ALU.pow),
                   reads=[rb, buf("nhalf")], writes=[rb])
            def f(h):
                r = None
                for hh in range(4):
                    o = vnf[:, hh * 128:(hh + 1) * 128] if last else vn_sb[:, i, hh * 128:(hh + 1) * 128]
                    r = h.scalar_tensor_tensor(out=o, in0=pf[:, hh * 128:(hh + 1) * 128], scalar=rs4[:, c, hh:hh + 1],
                                               in1=vg_bc[:, l, hh * 128:(hh + 1) * 128], op0=ALU.mult, op1=ALU.mult)
                return r
            def stage2():
                if not last:
                    S.emit(DVE, f, reads=[pfb, rb, buf(f"vg_bc{l}")], writes=[buf(f"vn{i}")])
                else:
                    S.emit(DVE, f, reads=[pfb, rb, buf(f"vg_bc{l}")], writes=[buf("vnf")])
                    S.emit(DVE, lambda h: h.tensor_copy(out=vn_sb[:, i, :], in_=vnf[:, :]), reads=[buf("vnf")],
                           writes=[buf(f"vn{i}")])
                    S.emit(SP, lambda h: h.dma_start(out=cvp_d[l], in_=vnf[:, :]), reads=[buf("vnf")], dma=_okey())
            return stage2

        vstate = {}

        def emit_v_proj(G, l, i):
            st = vstate.setdefault((G, l), {"pend": None, "q": []})
            pf, pfb = proj_tm(l, i, C_V)
            st["q"].append((i, pf, pfb))

        def emit_v_chain(G, l):
            st = vstate[(G, l)]
            for (i, pf, pfb) in st["q"]:
                st2 = v_tile(G, l, i, pf, pfb, last=(G == NG - 1 and i == TPG - 1))
                if st["pend"] is not None:
                    st["pend"]()
                st["pend"] = st2
            st["q"] = []

        def emit_v(G, l, i):
            emit_v_proj(G, l, i)
            emit_v_chain(G, l)

        def finish_v(G, l):
            st = vstate[(G, l)]
            emit_v_chain(G, l)
            if st["pend"] is not None:
                st["pend"]()
                st["pend"] = None

        def layer_group(G, l, before_wout=None, after_tile=None, mid_proj=None, after_proj=None):
            lastG = (G == NG - 1)
            if G == 0 and l == 0:
                prep_ws(0, "mask")
                prep_ws(1, "mask")
            if (G, l) not in vstate:
                for i in range(TPG):
                    emit_v(G, l, i)
            finish_v(G, l)
            if G == 0 and l == 0:
                for cb in WIN_ORDER:
                    ensure_fold(0, cb, only=True)
                prep_ws(0, "bias")
                prep_ws(1, "bias")
                for t in (4, 5, 7):
                    load_tile(t, deps=stage_ops + [ident_cast_op])
            for g in range(4):
                pf, pfb = proj_fm(l, C_AIN + g * 128)
                S.emit(ACT, lambda h, pf=pf, g=g: h.copy(out=a_sb[:, g % 2, HALO:AW], in_=pf[:, 0:GT]),
                       reads=[pfb], writes=[buf(f"a_sb{g % 2}")])
                pooling(G, l, g)
            for g in range(4):
                pf, pfb = proj_fm(l, C_AGATE + g * 128)
                S.emit(ACT, lambda h, pf=pf, g=g: h.activation(out=sg[:, g, :], in_=pf[:, 0:GT], func=AF.Silu),
                       reads=[pfb], writes=[buf(f"sg{g}")])
            if lastG:
                pf, pfb = proj_tm(l, TPG - 1, C_AIN)
                S.emit(ACT, lambda h, pf=pf: h.copy(out=pp_sb[96:128, :], in_=pf[96:128, :]), reads=[pfb],
                       writes=[buf("pp_sb")])
                S.emit(SP, lambda h: h.dma_start(out=pp_d[l], in_=pp_sb[128 - PB:128, :]), reads=[buf("pp_sb")],
                       dma=_okey())
            if mid_proj is not None:
                mid_proj()
            for g in range(4):
                pf, pfb = bankF.alloc()
                S.emit(PE, lambda h, pf=pf, g=g: h.matmul(out=pf[:, 0:GT], lhsT=wpool[:, l, g, :], rhs=d_sb[:, g, :],
                                                          start=True, stop=True),
                       reads=[buf(f"d{g}"), buf(f"wpool{l}")], writes=[pfb])
                S.emit(DVE, lambda h, pf=pf, g=g: h.scalar_tensor_tensor(
                    out=mix[:, g, :], in0=pf[:, 0:GT], scalar=psc[:, l, g, :], in1=sg[:, g, :],
                    op0=ALU.mult, op1=ALU.mult),
                    reads=[pfb, buf(f"psc{l}"), buf(f"sg{g}")], writes=[buf(f"mix{g}")])
                hh = g
                pf, pfb = proj_fm(l, C_BGATE + hh * 128)
                S.emit(ACT, lambda h, pf=pf, hh=hh: h.activation(out=sg[:, hh, :], in_=pf[:, 0:GT], func=AF.Silu),
                       reads=[pfb], writes=[buf(f"sg{hh}")])
            for hh in range(4):
                pf, pfb = proj_fm(l, C_U + hh * 128)
                S.emit(DVE, lambda h, pf=pf, hh=hh: h.tensor_tensor(out=sg[:, hh, :], in0=pf[:, 0:GT], in1=sg[:, hh, :],
                                                                    op=ALU.mult),
                       reads=[pfb, buf(f"sg{hh}")], writes=[buf(f"sg{hh}")])
            if after_proj is not None:
                after_proj()
            for hh in range(4):
                pf, pfb = bankF.alloc()

                def f(h, pf=pf, hh=hh):
                    r = None
                    h.matmul(out=pf[:, 0:GT].rearrange("p (c i) -> p c i", c=TPG), lhsT=ones2[:, :],
                             rhs=bias2[:, l, hh, :].unsqueeze(1).broadcast_to([2, TPG, 128]), start=True, stop=False)
                    for c in range(TPG):
                        r = h.matmul(out=pf[:, c * 128:(c + 1) * 128], lhsT=vn_sb[:, c, hh * 128:(hh + 1) * 128],
                                     rhs=wsT[:, l, hh, :], start=False, stop=(c == TPG - 1))
                    return r
                S.emit(PE, f, reads=[buf(f"vn{c}") for c in range(TPG)] + [buf(f"wsT{l}"), buf(f"bias2{l}"), buf("ones2")],
                       writes=[pfb])
                S.emit(DVE, lambda h, pf=pf, hh=hh: h.tensor_tensor(out=mix[:, 4 + hh, :], in0=pf[:, 0:GT],
                                                                    in1=sg[:, hh, :], op=ALU.mult),
                       reads=[pfb, buf(f"sg{hh}")], writes=[buf(f"mix{4 + hh}")])
            if before_wout is not None:
                before_wout()
            for i in range(TPG):
                for hf in range(2):
                    pf, pfb = bankF.alloc()

                    def f(h, pf=pf, i=i, hf=hf):
                        r = None
                        for k in range(8):
                            r = h.matmul(out=pf[:, :], lhsT=mix[:, k, i * 128:(i + 1) * 128],
                                         rhs=wout[l][:, k, hf * 512:(hf + 1) * 512], start=(k == 0), stop=(k == 7))
                        return r
                    S.emit(PE, f, reads=[buf(f"mix{k}") for k in range(8)] + [buf(f"wout{l}_{hf}")], writes=[pfb])
                    xa = xap(G, i)
                    S.emit(DVE, lambda h, pf=pf, xa=xa, hf=hf: h.tensor_tensor(
                        out=xa[:, hf * 512:(hf + 1) * 512], in0=pf[:, :], in1=xa[:, hf * 512:(hf + 1) * 512], op=ALU.add),
                        reads=[pfb, xbuf(G, i)], writes=[xbuf(G, i)])
                if after_tile is not None:
                    after_tile(i)

        def final_tile(G, i):
            xa = xap(G, i)
            c, rb = rstd_ops(xa, [xbuf(G, i)], D)
            S.emit(DVE, lambda h: h.scalar_tensor_tensor(out=xa, in0=xa, scalar=rsx[:, c:c + 1], in1=g_bc[:, :],
                                                         op0=ALU.mult, op1=ALU.mult),
                   reads=[xbuf(G, i), rb, buf("g_bc2")], writes=[xbuf(G, i)])
            t = G * TPG + i
            slot = t % NSLOT
            S.emit(SP, lambda h, s=slot, r=t * 128: h.dma_start(out=y_d[r:r + 128, :], in_=xg[:, s, :]),
                   reads=[buf(f"x{slot}")], dma=f"yo{slot}")
            if t + NSLOT < NT:
                load_tile(t + NSLOT)

        xsb = buf("xs")

        def sample_dmas(l):
            for t in range(2):
                S.emit(SP, lambda h, l=l, t=t: h.dma_start(
                    out=st_sb[:, t, :], in_=st_d[l][8 * t:8 * t + 8].rearrange("b r c -> (b r) c")),
                    writes=[buf(f"st_sb{t}")], dma=f"st{l}_{t}")

        def sample_inputs():
            S.emit(SP, lambda h: h.dma_start(out=xs_sb[:, :], in_=xs_d[:, :]), writes=[xsb], dma="xsl")
            sample_dmas(0)
            for l in range(2):
                S.emit(SP, lambda h, l=l: h.dma_start(out=pso_d[l][:, 0:PB - 1, :], in_=st_d[l][:, 1:PB, :]), dma=_okey())

        def transpose_s(src, srcb, nk, dst, dstb):
            pt, ptb = bankT.alloc()

            def f(h):
                r = None
                for k in range(nk):
                    r = h.transpose(out=pt[:, k * NS:(k + 1) * NS], in_=src[:, k * 128:(k + 1) * 128],
                                    identity=ident[0:NS, 0:NS])
                return r
            S.emit(PE, f, reads=[srcb, buf("ident")], writes=[ptb])
            S.emit(DVE, lambda h: h.tensor_copy(out=dst, in_=pt[:, 0:nk * NS].rearrange("p (k t) -> p k t", k=nk)),
                   reads=[ptb], writes=[dstb])

        def proj_s(l, col0):
            pf, pfb = bankF.alloc()

            def f(h):
                r = None
                for k in range(8):
                    r = h.matmul(out=pf[0:NS, :], lhsT=hTs[:, k, :], rhs=win[l][:, k, col0:col0 + 512],
                                 start=(k == 0), stop=(k == 7))
                return r
            S.emit(PE, f, reads=[buf("hTs"), buf(f"win{l}_{col0}")], writes=[pfb])
            return pf, pfb

        def sample_front(l):
            c, rb = rstd_ops(xs_sb[:, :], [xsb], D, nparts=NS)
            S.emit(DVE, lambda h, c=c, l=l: h.tensor_scalar(out=hbs[:, :], in0=xs_sb[:, :], scalar1=rsx[0:NS, c:c + 1],
                                                            scalar2=None, op0=ALU.mult),
                   reads=[xsb, rb], writes=[buf("hbs")])
            transpose_s(hbs, buf("hbs"), 8, hTs[:, :, :], buf("hTs"))

        def sample_rest(l, mid_hook=None):
            S.emit(SP, lambda h, l=l: h.dma_start(out=psc_bc[:, :], in_=psc_d[l:l + 1, :].broadcast_to([NS, WA])),
                   writes=[buf("psc_bc")], dma=f"pscbc{l}")
            pa, pab = proj_s(l, C_AIN)
            pS, pSb = bankF.alloc()

            def f_sel(h, pS=pS):
                r = None
                for g in range(4):
                    for t in range(2):
                        r = h.matmul(out=pS[0:NS, g * 128:(g + 1) * 128], lhsT=sel[:, t, g, :],
                                     rhs=st_sb[:, t, g * 128:(g + 1) * 128], start=(t == 0), stop=(t == 1))
                return r
            S.emit(PE, f_sel, reads=[buf("st_sb0"), buf("st_sb1"), buf("sel")], writes=[pSb])
            if l == 0:
                sample_dmas(1)
            pv, pvb = proj_s(l, C_V)
            pg, pgb = proj_s(l, C_AGATE)
            pbg, pbgb = proj_s(l, C_BGATE)
            pu, pub = proj_s(l, C_U)
            S.emit(ACT, lambda h, pf=pa: h.copy(out=ains[:, :], in_=pf[0:NS, :]), reads=[pab], writes=[buf("ains")])
            S.emit(SP, lambda h, l=l: h.dma_start(out=pso_d[l][:, PB - 1, :], in_=ains[:, :]), reads=[buf("ains")],
                   dma=_okey())
            for g in range(4):
                w = WINDOWS[g]
                S.emit(DVE, lambda h, g=g, w=w: h.tensor_scalar(out=tmp_s[:, g * 128:(g + 1) * 128],
                                                                in0=ains[:, g * 128:(g + 1) * 128], scalar1=(1.0 / w - 1.0),
                                                                scalar2=None, op0=ALU.mult),
                       reads=[buf("ains")], writes=[buf("tmp_s")])
                S.emit(DVE, lambda h, g=g, w=w, pS=pS: h.scalar_tensor_tensor(
                    out=ds_s[:, g * 128:(g + 1) * 128], in0=pS[0:NS, g * 128:(g + 1) * 128], scalar=1.0 / w,
                    in1=tmp_s[:, g * 128:(g + 1) * 128], op0=ALU.mult, op1=ALU.add),
                    reads=[pSb, buf("tmp_s")], writes=[buf("ds_s")])
            S.emit(ACT, lambda h, pf=pv: h.activation(out=s_s[:, :], in_=pf[0:NS, :], func=AF.Square), reads=[pvb],
                   writes=[buf("s_s")])
            c4 = newcol4()
            sb_ = buf(f"ss4_{c4}")
            S.emit(DVE, lambda h, c4=c4: h.tensor_reduce(out=ss4[0:NS, c4, :],
                                                         in_=s_s[:, :].rearrange("p (h d) -> p h d", h=4),
                                                         axis=AX.X, op=ALU.add),
                   reads=[buf("s_s"), buf("ss4")], writes=[sb_])
            rb4 = buf(f"rs4_{c4}")
            S.emit(DVE, lambda h, c4=c4: h.tensor_scalar(out=rs4[0:NS, c4, :], in0=ss4[0:NS, c4, :], scalar1=1.0 / 128,
                                                         scalar2=EPS, op0=ALU.mult, op1=ALU.add), reads=[sb_], writes=[rb4])
            S.emit(POOL, lambda h, c4=c4: h.tensor_tensor(out=rs4[0:NS, c4, :], in0=rs4[0:NS, c4, :], in1=nhalf[0:NS, :],
                                                          op=ALU.pow), reads=[rb4, buf("nhalf")], writes=[rb4])
            S.emit(ACT, lambda h, pf=pg: h.activation(out=sga_s[:, :], in_=pf[0:NS, :], func=AF.Silu), reads=[pgb],
                   writes=[buf("sga_s")])
            S.emit(ACT, lambda h, pf=pbg: h.activation(out=sgb_s[:, :], in_=pf[0:NS, :], func=AF.Silu), reads=[pbgb],
                   writes=[buf("sgb_s")])
            def f_vn(h, pf=pv, c4=c4, l=l):
                r = None
                for hh in range(4):
                    r = h.scalar_tensor_tensor(out=vns[:, hh * 128:(hh + 1) * 128], in0=pf[0:NS, hh * 128:(hh + 1) * 128],
                                               scalar=rs4[0:NS, c4, hh:hh + 1], in1=vg_bc[0:NS, l, hh * 128:(hh + 1) * 128],
                                               op0=ALU.mult, op1=ALU.mult)
                return r
            S.emit(DVE, f_vn, reads=[pvb, rb4, buf(f"vg_bc{l}")], writes=[buf("vns")])
            S.emit(SP, lambda h, l=l: h.dma_start(out=cvs_d[l], in_=vns[:, :]), reads=[buf("vns")], dma=_okey())

            def f_s(h, l=l):
                r = None
                for hh in range(4):
                    r = h.tensor_scalar(out=s_s[:, hh * 128:(hh + 1) * 128], in0=vns[:, hh * 128:(hh + 1) * 128],
                                        scalar1=ws00[:, l, hh, :], scalar2=bs0[:, l, hh, :], op0=ALU.mult, op1=ALU.add)
                return r
            S.emit(DVE, f_s, reads=[buf("vns"), buf(f"ws00{l}"), buf(f"bs0{l}")], writes=[buf("s_s")])
            transpose_s(ds_s, buf("ds_s"), 4, dsT[:, :, :], buf("dsT"))
            S.emit(DVE, lambda h, pf=pu: h.tensor_tensor(out=sgb_s[:, :], in0=pf[0:NS, :], in1=sgb_s[:, :], op=ALU.mult),
                   reads=[pub, buf("sgb_s")], writes=[buf("sgb_s")])
            S.emit(DVE, lambda h: h.tensor_tensor(out=mixs[:, WA:D], in0=sgb_s[:, :], in1=s_s[:, :], op=ALU.mult),
                   reads=[buf("sgb_s"), buf("s_s")], writes=[buf("mixs")])
            pP, pPb = bankF.alloc()

            def f_pl(h, pP=pP, l=l):
                r = None
                for g in range(4):
                    r = h.matmul(out=pP[0:NS, g * 128:(g + 1) * 128], lhsT=dsT[:, g, :], rhs=wpool[:, l, g, :],
                                 start=True, stop=True)
                return r
            S.emit(PE, f_pl, reads=[buf("dsT"), buf(f"wpool{l}")], writes=[pPb])
            S.emit(DVE, lambda h, pP=pP, l=l: h.tensor_tensor(out=tmp_s[:, :], in0=pP[0:NS, :], in1=psc_bc[:, :],
                                                              op=ALU.mult),
                   reads=[pPb, buf("psc_bc")], writes=[buf("tmp_s")])
            S.emit(DVE, lambda h: h.tensor_tensor(out=mixs[:, 0:WA], in0=tmp_s[:, :], in1=sga_s[:, :], op=ALU.mult),
                   reads=[buf("tmp_s"), buf("sga_s")], writes=[buf("mixs")])
            transpose_s(mixs, buf("mixs"), 8, mixTs[:, :, :], buf("mixTs"))
            if mid_hook is not None:
                mid_hook()
            for hf in range(2):
                pf, pfb = bankF.alloc()

                def f_o(h, pf=pf, hf=hf, l=l):
                    r = None
                    for k in range(8):
                        r = h.matmul(out=pf[0:NS, :], lhsT=mixTs[:, k, :], rhs=wout[l][:, k, hf * 512:(hf + 1) * 512],
                                     start=(k == 0), stop=(k == 7))
                    return r
                S.emit(PE, f_o, reads=[buf("mixTs"), buf(f"wout{l}_{hf}")], writes=[pfb])
                S.emit(DVE, lambda h, pf=pf, hf=hf: h.tensor_tensor(out=xs_sb[:, hf * 512:(hf + 1) * 512], in0=pf[0:NS, :],
                                                                    in1=xs_sb[:, hf * 512:(hf + 1) * 512], op=ALU.add),
                       reads=[pfb, xsb], writes=[xsb])
        def sample_final():
            c, rb = rstd_ops(xs_sb[:, :], [xsb], D, nparts=NS)
            S.emit(DVE, lambda h: h.scalar_tensor_tensor(out=xs_sb[:, :], in0=xs_sb[:, :], scalar=rsx[0:NS, c:c + 1],
                                                         in1=g_bc[0:NS, :], op0=ALU.mult, op1=ALU.mult),
                   reads=[xsb, rb, buf("g_bc2")], writes=[xsb])
            S.emit(SP, lambda h: h.dma_start(out=ys_d[:, :], in_=xs_sb[:, :]), reads=[xsb], dma=_okey())

        def finals_batched(G, load_next=False, split=False):
            cols = []
            for i in range(TPG):
                c = newcol()
                cb_ = buf(f"ssx_{c}")
                xa = xap(G, i)
                S.emit(ACT, lambda h, xa=xa, c=c: h.activation(out=junk[:, 0:D], in_=xa, func=AF.Square,
                                                               accum_out=ssx[:, c:c + 1]),
                       reads=[xbuf(G, i), buf("ssx")], writes=[cb_, buf("vscr")])
                cols.append((c, cb_, buf(f"rsx_{c}")))
            for i, (c, cb_, rb) in enumerate(cols):
                S.emit(DVE, lambda h, c=c: h.tensor_scalar(out=rsx[:, c:c + 1], in0=ssx[:, c:c + 1], scalar1=1.0 / D,
                                                           scalar2=EPS, op0=ALU.mult, op1=ALU.add), reads=[cb_], writes=[rb])
            for i, (c, cb_, rb) in enumerate(cols):
                S.emit(POOL, lambda h, c=c: h.tensor_tensor(out=rsx[:, c:c + 1], in0=rsx[:, c:c + 1], in1=nhalf[:, 0:1],
                                                            op=ALU.pow), reads=[rb, buf("nhalf")], writes=[rb])
            def stage_b(only=None):
                for i, (c, cb_, rb) in enumerate(cols):
                    if only is not None and i != only:
                        continue
                    xa = xap(G, i)
                    S.emit(DVE, lambda h, xa=xa, c=c: h.scalar_tensor_tensor(out=xa, in0=xa, scalar=rsx[:, c:c + 1],
                                                                             in1=g_bc[:, :], op0=ALU.mult, op1=ALU.mult),
                           reads=[xbuf(G, i), rb, buf("g_bc2")], writes=[xbuf(G, i)])
                    t = G * TPG + i
                    slot = t % NSLOT
                    S.emit(SP, lambda h, s_=slot, r=t * 128: h.dma_start(out=y_d[r:r + 128, :], in_=xg[:, s_, :]),
                           reads=[buf(f"x{slot}")], dma=f"yo{slot}")
                    if load_next and t + NSLOT < NT:
                        load_tile(t + NSLOT)
            if split:
                return stage_b
            stage_b()

        load_tile(6)
        for i in range(NHB):
            norm_pre(0, 0, i)
        for i in range(TPG):
            norm_tr(0, 0, i)
            if i + NHB < TPG:
                norm_pre(0, 0, i + NHB)
        for G in range(NG):
            def after0(i, G=G):
                norm_pre(G, 1, i)
                if i >= 1:
                    norm_tr(G, 1, i - 1)
                if i == TPG - 1:
                    for j in range(TPG - 1):
                        emit_v_proj(G, 1, j)
                    norm_tr(G, 1, i)
                    emit_v_proj(G, 1, i)
                    emit_v_chain(G, 1)
            fin_prev = {}

            def bw0(G=G, fin_prev=fin_prev):
                fin_prev["b"] = finals_batched(G - 1, load_next=True, split=True)

            def after0w(i, G=G, fin_prev=fin_prev):
                after0(i)
                if "b" in fin_prev:
                    fin_prev["b"](only=i)
            layer_group(G, 0, after_tile=after0w, before_wout=(bw0 if G > 0 else None))

            nxt = G + 1 < NG

            def after1(i, G=G, nxt=nxt):
                if not nxt:
                    return
                norm_tr(G + 1, 0, i)
                if i == 0:
                    norm_pre(G + 1, 0, 3)

            def aft1(G=G):
                norm_pre(G + 1, 0, 0)
                norm_pre(G + 1, 0, 1)
                norm_pre(G + 1, 0, 2)
            if not nxt:
                sample_inputs()
            layer_group(G, 1, after_tile=after1, before_wout=(aft1 if nxt else None))

        sample_front(0)
        fin_b = finals_batched(NG - 1, split=True)
        sample_rest(0, mid_hook=fin_b)
        sample_front(1)
        sample_rest(1)
        sample_final()

        fin_deps = [op for op in S.ops[SP] if op.dma is not None and (op.dma.startswith("out") or op.dma.startswith("yo"))]
        S.emit(SP, lambda h: None, deps=fin_deps)

        keys = S.resolve()
        sems = {k: es.enter_context(nc.semaphore(f"s_{k}")) for k in keys}
        with nc.allow_non_contiguous_dma(reason="small strided parameter / state loads"), nc.Block() as block:
            @block.sync
            def _(h):
                run_sync(S, h, sems)

            @block.tensor
            def _(h):
                S.run_engine(PE, h, sems)

            @block.scalar
            def _(h):
                S.run_engine(ACT, h, sems)

            @block.vector
            def _(h):
                S.run_engine(DVE, h, sems)

            @block.gpsimd
            def _(h):
                S.run_engine(POOL, h, sems)
    return nc


def run_sync(S, h, sems):
    waited = {}
    ops = S.ops[SP]
    for n, op in enumerate(ops):
        for d in op.deps:
            key, val = d.tok
            if waited.get(key, 0) < val:
                h.wait_ge(sems[key], val)
                waited[key] = val
        if n == len(ops) - 1:
            break
        ins = op.fn(h)
        if op.dma is not None:
            ins.then_inc(sems[op.dma], 16)


_NC_CACHE = {}


def _consts():
    ident = np.eye(128, dtype=np.float32)
    tril = np.tril(np.ones((128, 128), dtype=np.float32))
    inv = np.zeros((4, 16), dtype=np.float32)
    for g, w in enumerate(WINDOWS):
        for t in range(16):
            inv[g, t] = 1.0 / min(t + 1, w)
    sel = np.zeros((120, 2, 4, NS), dtype=np.float32)
    for t in range(2):
        for bb in range(8):
            for r in range(PB):
                for g, w in enumerate(WINDOWS):
                    if r >= 16 - w:
                        sel[bb * PB + r, t, g, 8 * t + bb] = 1.0
    m01 = np.array([[1.0, 0.0], [0.0, 1.0]], dtype=np.float32)
    return dict(c_ident=ident, c_tril=tril, c_invcnt=inv.reshape(1, 64), c_sel=sel.reshape(120, 128), c_m01=m01)


def kernel(x_prompt, x_sample, state_pool, norm_g, w_in, w_pool, pool_scale, v_norm_g, w_s, b_s, w_out,
           final_norm_g):
    f = lambda a: np.ascontiguousarray(np.asarray(a, dtype=np.float32))
    x_prompt, x_sample, state_pool = f(x_prompt), f(x_sample), f(state_pool)
    shared = dict(norm_g=f(norm_g), w_in=f(w_in), w_pool=f(w_pool), pool_scale=f(pool_scale), v_norm_g=f(v_norm_g),
                  w_s=f(w_s), b_s=f(b_s), w_out=f(w_out), fng=f(final_norm_g).reshape(1, D))
    shared["gT_h"] = np.ascontiguousarray(shared["norm_g"].reshape(2, 8, 128).transpose(2, 0, 1).reshape(128, 16))
    shared["pscT_h"] = np.ascontiguousarray(shared["pool_scale"].reshape(2, 4, 128).transpose(2, 0, 1).reshape(128, 8))
    shared["ws00_h"] = np.ascontiguousarray(shared["w_s"][:, :, 0, 0].reshape(1, 8))
    shared["bs0_h"] = np.ascontiguousarray(shared["b_s"][:, :, 0].reshape(1, 8))
    shared.update(_consts())
    if "nc" not in _NC_CACHE:
        _NC_CACHE["nc"] = build_program()
    nc = _NC_CACHE["nc"]
    in_maps = []
    for c in range(NCORES):
        m = dict(shared)
        m["x"] = np.ascontiguousarray(x_prompt[c])
        m["xs"] = np.ascontiguousarray(x_sample[c * NS:(c + 1) * NS, 0, :])
        m["st"] = np.ascontiguousarray(state_pool[:, c * NS:(c + 1) * NS])
        in_maps.append(m)
    res = run_bass_kernel_spmd(nc, in_maps, core_ids=list(range(NCORES)))
    R = res.results
    y_prompt = np.stack([R[c]["y"] for c in range(NCORES)], axis=0)
    y_sample = np.concatenate([R[c]["ys"] for c in range(NCORES)], axis=0)[:, None, :]
    pool_prompt = np.stack([R[c]["pp"] for c in range(NCORES)], axis=1)
    pool_sample = np.concatenate([R[c]["pso"] for c in range(NCORES)], axis=1)
    cv_prompt = np.stack([R[c]["cvp"] for c in range(NCORES)], axis=1)
    cv_sample = np.concatenate([R[c]["cvs"] for c in range(NCORES)], axis=1)[:, :, None, :]
    return (y_prompt.astype(np.float32), y_sample.astype(np.float32), pool_prompt.astype(np.float32),
            pool_sample.astype(np.float32), cv_prompt.astype(np.float32), cv_sample.astype(np.float32))
```

```python
import numpy as np
from contextlib import ExitStack
import concourse.bass as bass
import concourse.mybir as mybir
from concourse.bass_utils import run_bass_kernel_spmd

F32 = mybir.dt.float32
BF16 = mybir.dt.bfloat16
AF = mybir.ActivationFunctionType
ALU = mybir.AluOpType
AX = mybir.AxisListType

PE, ACT, DVE, POOL, SP = "tensor", "scalar", "vector", "gpsimd", "sync"
ENGS = (PE, ACT, DVE, POOL, SP)

NCORES = 8
D = 1024
SEQ = 2048
WA = 512
WB = 512
DIN = 2560
NS = 16
PB = 15
EPS = 1e-6
TPG = 4
GT = TPG * 128
NG = SEQ // GT
NSLOT = 8
NT = SEQ // 128
NHB = 3
HALO = 16
AW = HALO + GT
C_AIN, C_AGATE, C_U, C_V, C_BGATE = 0, 512, 1024, 1536, 2048
WINDOWS = (2, 4, 8, 16)


def _flat(xs):
    out = []
    for x in xs:
        if isinstance(x, (list, tuple)):
            out.extend(_flat(x))
        else:
            out.append(x)
    return out


class Buf:
    __slots__ = ("name", "writer", "readers")

    def __init__(self, name):
        self.name = name
        self.writer = None
        self.readers = []


class Op:
    __slots__ = ("eng", "fn", "deps", "dma", "tok", "used", "group")

    def __init__(self, eng, fn, dma=None, group=False):
        self.eng = eng
        self.fn = fn
        self.deps = []
        self.dma = dma
        self.group = group
        self.tok = None
        self.used = False


class Sched:
    def __init__(self):
        self.ops = {e: [] for e in ENGS}
        self.all = []

    def emit(self, eng, fn, reads=(), writes=(), deps=(), dma=None, group=False):
        op = Op(eng, fn, dma, group)
        reads = _flat(reads)
        writes = _flat(writes)
        dd = []
        for b in reads:
            if b.writer is not None:
                dd.append(b.writer)
        for b in writes:
            dd.extend(b.readers)
            if b.writer is not None:
                dd.append(b.writer)
        dd.extend(d for d in deps if d is not None)
        seen = set()
        for d in dd:
            if d is op or id(d) in seen:
                continue
            if d.eng == PE and eng == PE and d.dma is None:
                continue
            seen.add(id(d))
            d.used = True
            op.deps.append(d)
        for b in reads:
            b.readers.append(op)
        for b in writes:
            b.writer = op
            b.readers = []
        self.ops[eng].append(op)
        self.all.append(op)
        return op

    def resolve(self):
        keys = {}
        dma_cnt = {}
        group_ops = {}
        for e in ENGS:
            cnt = 0
            for op in self.ops[e]:
                if op.dma is not None:
                    dma_cnt[op.dma] = dma_cnt.get(op.dma, 0) + 16
                    op.tok = (op.dma, dma_cnt[op.dma])
                    keys[op.dma] = None
                    if op.group:
                        group_ops.setdefault(op.dma, []).append(op)
                elif op.used:
                    cnt += 1
                    op.tok = (e, cnt)
                    keys[e] = None
        for k, ops in group_ops.items():
            for op in ops:
                op.tok = (k, dma_cnt[k])
        return keys

    def run_engine(self, e, h, sems):
        waited = {}
        for op in self.ops[e]:
            for d in op.deps:
                key, val = d.tok
                if waited.get(key, 0) < val:
                    h.wait_ge(sems[key], val)
                    waited[key] = val
            ins = op.fn(h)
            if op.dma is not None:
                ins.then_inc(sems[op.dma], 16)
            elif op.used:
                ins.then_inc(sems[e], 1)


class Banks:
    def __init__(self, tensors, tag):
        self.t = tensors
        self.bufs = [Buf(f"{tag}{i}") for i in range(len(tensors))]
        self.nxt = 0

    def alloc(self):
        i = self.nxt
        self.nxt = (self.nxt + 1) % len(self.t)
        return self.t[i], self.bufs[i]


def build_program():
    nc = bass.Bass("TRN2", target_bir_lowering=False)
    S = Sched()

    def din(name, shape):
        return nc.dram_tensor(name, list(shape), F32, kind="ExternalInput").ap()

    def dout(name, shape):
        return nc.dram_tensor(name, list(shape), F32, kind="ExternalOutput").ap()

    x_d = din("x", [SEQ, D])
    xs_d = din("xs", [NS, D])
    st_d = din("st", [2, NS, PB, WA])
    ng_d = din("norm_g", [2, D])
    win_d = din("w_in", [2, D, DIN])
    wpool_d = din("w_pool", [2, 4, 128, 128])
    psc_d = din("pool_scale", [2, WA])
    vng_d = din("v_norm_g", [2, WB])
    ws_d = din("w_s", [2, 4, 128, 128])
    bs_d = din("b_s", [2, 4, 128])
    wout_d = din("w_out", [2, D, D])
    fng_d = din("fng", [1, D])
    cid_d = din("c_ident", [128, 128])
    ctril_d = din("c_tril", [128, 128])
    cinv_d = din("c_invcnt", [1, 64])
    csel_d = din("c_sel", [120, 128])
    cm01_d = din("c_m01", [2, 2])
    gT_d = din("gT_h", [128, 16])
    pscT_d = din("pscT_h", [128, 8])
    ws00_d = din("ws00_h", [1, 8])
    bs0_d = din("bs0_h", [1, 8])

    y_d = dout("y", [SEQ, D])
    ys_d = dout("ys", [NS, D])
    pp_d = dout("pp", [2, PB, WA])
    pso_d = dout("pso", [2, NS, PB, WA])
    cvp_d = dout("cvp", [2, 128, WB])
    cvs_d = dout("cvs", [2, NS, WB])

    with ExitStack() as es:
        def sb(name, shape, dt):
            return es.enter_context(nc.sbuf_tensor(name, list(shape), dt))

        def ps(name, shape, dt):
            return es.enter_context(nc.psum_tensor(name, list(shape), dt))

        win = [sb(f"win{l}", [128, 8, DIN], BF16) for l in range(2)]
        wout = [sb(f"wout{l}", [128, 8, D], BF16) for l in range(2)]
        wpool = sb("wpool", [128, 2, 4, 128], BF16)
        wsT = sb("wsT", [128, 2, 4, 128], BF16)
        xg = sb("xg", [128, NSLOT, D], F32)
        hb = sb("hb", [128, NHB, D], BF16)
        hT = sb("hT", [128, 8, GT], BF16)
        a_sb = sb("a_sb", [128, 2, AW], F32)
        halo = sb("halo", [128, 2, 4, HALO], F32)
        Tt = sb("Tt", [128, 2, AW], F32)
        d_sb = sb("d_sb", [128, 4, GT], BF16)
        sg = sb("sg", [128, 4, GT], BF16)
        mix = sb("mix", [128, 8, GT], BF16)
        vn_sb = sb("vn_sb", [128, TPG, WB], BF16)
        vscr = sb("vscr", [128, WB], F32)
        junk = vscr[:, :].bitcast(BF16)
        vnf = sb("vnf", [128, WB], F32)
        pp_sb_ = vnf
        pp_sb = pp_sb_
        g_bc = sb("g_bc", [128, D], F32)
        gT = sb("gT", [128, 2, 8], F32)
        vg_bc = sb("vg_bc", [128, 2, WB], F32)
        psc = sb("psc", [128, 2, 4, 1], F32)
        ident = sb("ident", [128, 128], BF16)
        identf = xg[:, 5, 512:640]
        tril = xg[:, 5, 640:768]
        wsm = xg[:, 5, 768:1024].bitcast(BF16).rearrange("p (h j) -> p h j", h=4)
        invcnt = sb("invcnt", [128, 4, 16], F32)
        fix_t = sb("fix_t", [128, 16], F32)
        nhalf = sb("nhalf", [128, 4], F32)
        epsb = sb("epsb", [128, 1], F32)
        ones2 = sb("ones2", [2, 128], BF16)
        fl = lambda ap: ap.rearrange("p a b -> p (a b)")
        r3 = lambda ap: ap.rearrange("p (h i) -> p h i", h=4)
        bsf_l = [r3(xg[0:2, 4, 0:512]), r3(xg[0:2, 4, 512:1024])]
        bhif = r3(Tt[0:2, 0, 0:512])
        bhi = r3(hb[0:2, 2, 0:512])
        blo = r3(hb[0:2, 2, 512:1024])
        bias2 = sb("bias2", [2, 2, 4, 128], BF16)
        m01 = sb("m01", [2, 2], F32)
        ssx = sb("ssx", [128, 4 * 2 * SEQ // 128 // 2 + 64], F32)
        rsx = ssx
        ss4 = sb("ss4", [128, 2 * (SEQ // 128) + 4, 4], F32)
        rs4 = ss4
        xs_sb = xg[0:NS, 0, :]
        hbs = hb[0:NS, 0, :]
        hTs = sb("hTs", [128, 8, NS], BF16)
        st_sb = xg[0:120, 1, :].rearrange("p (t c) -> p t c", t=2)
        sel = sb("sel", [120, 2, 4, NS], F32)
        ains = pp_sb_[0:NS, :]
        ds_s = d_sb[0:NS, 0, :]
        dsT = sb("dsT", [128, 4, NS], BF16)
        tmp_s = Tt[0:NS, 0, 0:512]
        sga_s = a_sb[0:NS, 0, 0:512]
        sgb_s = a_sb[0:NS, 1, 0:512]
        s_s = Tt[0:NS, 1, 0:512]
        vns = vnf[0:NS, :]
        mixs = hb[0:NS, 1, :]
        mixTs = hTs
        psc_bc = vscr[0:NS, :]
        ws00 = sb("ws00", [NS, 2, 4, 1], F32)
        bs0 = sb("bs0", [NS, 2, 4, 1], F32)

        pT = [ps(f"pT{i}", [128, D], BF16) for i in range(2)]
        pF = [ps(f"pF{i}", [128, 512], F32) for i in range(6)]
        bankT = Banks(pT, "pT")
        bankF = Banks(pF, "pF")

        B = {}

        ALIAS = {
            "bhif": ["T0"], "bhi": ["hb2"], "blo": ["hb2"], "mixTs": ["hTs"], "pp_sb": ["vnf"],
            "hbs": ["hb0"], "st_sb0": ["x1"], "st_sb1": ["x1"], "ains": ["pp_sb"], "ds_s": ["d0"],
            "tmp_s": ["T0"], "sga_s": ["a_sb0", "a_halo0"], "sgb_s": ["a_sb1", "a_halo1"], "s_s": ["T1"],
            "vns": ["vnf"], "mixs": ["hb1"], "xs": ["x0"], "psc_bc": ["vscr"],
        }

        def buf(name):
            if name in ALIAS:
                return [buf(n) for n in ALIAS[name]]
            if name not in B:
                B[name] = Buf(name)
            return B[name]

        col_ctr = [0]

        def newcol():
            c = col_ctr[0]
            col_ctr[0] += 1
            return c

        okey_ctr = [0]

        def _okey():
            okey_ctr[0] += 1
            return f"out{okey_ctr[0]}"

        col4_ctr = [0]

        def newcol4():
            c = col4_ctr[0]
            col4_ctr[0] += 1
            return c

        stage_ops = []

        def cdma(out_ap, in_ap, wbuf, eng=SP):
            return S.emit(eng, lambda h, o=out_ap, i=in_ap: h.dma_start(out=o, in_=i),
                          writes=[wbuf], dma="const", group=True)

        def load_tile(t, deps=()):
            slot = t % NSLOT
            r0 = t * 128
            return S.emit(SP, lambda h, s=slot, r=r0: h.dma_start(out=xg[:, s, :], in_=x_d[r:r + 128, :]),
                          writes=[buf(f"x{slot}")], dma=f"xl{slot}", deps=deps)

        S.emit(SP, lambda h: h.dma_start(out=identf[:, :], in_=cid_d[:, :]), writes=[buf("identf")], dma="cident")
        for l in range(2):
            S.emit(SP, lambda h, l=l: h.dma_start(out=gT[:, l, :], in_=gT_d[:, l * 8:(l + 1) * 8]),
                   writes=[buf(f"gT{l}")], dma=f"cgT{l}")
        for t in range(4):
            load_tile(t)
        def prep_dma(l):
            stage_ops.append(S.emit(SP, lambda h, l=l: h.dma_start(
                out=bsf_l[l][:, :, :], in_=bs_d[l].rearrange("(o h) i -> o h i", o=1).broadcast_to([2, 4, 128])),
                writes=[buf(f"bsf{l}")], dma=f"bsf{l}"))

        ckey = [0]

        def cdma1(out_ap, in_ap, wbuf):
            ckey[0] += 1
            return S.emit(SP, lambda h, o=out_ap, i=in_ap: h.dma_start(out=o, in_=i), writes=[wbuf], dma=f"c{ckey[0]}")

        cdma1(vg_bc[:, 0, :], vng_d[0:1, :].broadcast_to([128, WB]), buf("vg_bc0"))
        cdma1(tril[:, :], ctril_d[:, :], buf("tril"))
        stage_ops.append(cdma1(xg[:, 5, 0:512].rearrange("p (h j) -> p h j", h=4), ws_d[0].rearrange("h i j -> i h j"),
                               buf("wsf0")))
        stage_ops.append(cdma1(xg[:, 7, 0:512].rearrange("p (h j) -> p h j", h=4), ws_d[1].rearrange("h i j -> i h j"),
                               buf("wsf1")))
        cdma1(m01[:, :], cm01_d[:, :], buf("m01"))
        prep_dma(0)
        prep_dma(1)
        cdma1(invcnt[:, :, :].rearrange("p g t -> p (g t)"), cinv_d[0:1, :].broadcast_to([128, 64]), buf("invcnt"))
        cdma1(psc[:, 0, :, :].rearrange("p g o -> p (g o)"), pscT_d[:, 0:4], buf("psc0"))
        cdma1(vg_bc[:, 1, :], vng_d[1:2, :].broadcast_to([128, WB]), buf("vg_bc1"))
        cdma1(psc[:, 1, :, :].rearrange("p g o -> p (g o)"), pscT_d[:, 4:8], buf("psc1"))
        cdma1(g_bc[:, :], fng_d[0:1, :].broadcast_to([128, D]), buf("g_bc2"))
        cdma(sel[:, :, :, :].rearrange("p t g b -> p (t g b)"), csel_d[:, :], buf("sel"))
        for l in range(2):
            cdma(ws00[:, l, :, :].rearrange("p g o -> p (g o)"), ws00_d[0:1, l * 4:(l + 1) * 4].broadcast_to([NS, 4]),
                 buf(f"ws00{l}"))
            cdma(bs0[:, l, :, :].rearrange("p g o -> p (g o)"), bs0_d[0:1, l * 4:(l + 1) * 4].broadcast_to([NS, 4]),
                 buf(f"bs0{l}"))
        def wdma(out_ap, in_ap, wbuf, key):
            return S.emit(POOL, lambda h, o=out_ap, i=in_ap: h.dma_start(out=o, in_=i), writes=[wbuf], dma=key)

        WIN_ORDER = (C_V, C_AIN, C_AGATE, C_BGATE, C_U)
        fold_list = []
        wlist = []
        for l in range(2):
            for cb in WIN_ORDER:
                wlist.append((win[l][:, :, cb:cb + 512],
                              win_d[l][:, cb:cb + 512].rearrange("(k p) e -> p k e", p=128),
                              f"win{l}_{cb}"))
            wlist.append((wpool[:, l, :, :], wpool_d[l].rearrange("g c d -> c g d"), f"wpool{l}"))
            for hf in range(2):
                wlist.append((wout[l][:, :, hf * 512:(hf + 1) * 512],
                              wout_d[l][:, hf * 512:(hf + 1) * 512].rearrange("(k p) e -> p k e", p=128),
                              f"wout{l}_{hf}"))

        def issue_weights(n):
            for _ in range(n):
                if wlist:
                    o, i, key = wlist.pop(0)
                    wdma(o, i, buf(key), key)

        issue_weights(100)

        def fold_g(l, cb):
            def f(h):
                r = None
                for k in range(8):
                    r = h.tensor_scalar(out=win[l][:, k, cb:cb + 512], in0=win[l][:, k, cb:cb + 512],
                                        scalar1=gT[:, l, k:k + 1], scalar2=None, op0=ALU.mult)
                return r
            S.emit(DVE, f, reads=[buf(f"win{l}_{cb}"), buf(f"gT{l}")], writes=[buf(f"win{l}_{cb}")])

        S.emit(DVE, lambda h: h.memset(ones2[:, :], 1.0), writes=[buf("ones2")])
        S.emit(DVE, lambda h: h.memset(ssx[:, :], 0.0), writes=[buf("ssx")])
        S.emit(DVE, lambda h: h.memset(ss4[:, :, :], 0.0), writes=[buf("ss4")])
        S.emit(POOL, lambda h: h.memset(halo[:, :, :, :], 0.0), writes=[buf(f"halo{l}_{g}") for l in range(2) for g in range(4)])
        S.emit(DVE, lambda h: h.memset(nhalf[:, :], -0.5), writes=[buf("nhalf")])
        S.emit(DVE, lambda h: h.memset(epsb[:, :], EPS), writes=[buf("epsb")])
        ident_cast_op = S.emit(DVE, lambda h: h.tensor_copy(out=ident[:, :], in_=identf[:, :]), reads=[buf("identf")],
                               writes=[buf("ident")])

        def prep_ws(l, part):
            bsf = bsf_l[l]
            btmp = bsf
            if l == 0:
                wsf_v = xg[:, 5, 0:512].rearrange("p (h j) -> p h j", h=4)
                wsfb = buf("wsf0")
            else:
                wsf_v = xg[:, 7, 0:512].rearrange("p (h j) -> p h j", h=4)
                wsfb = buf("wsf1")
            wsm_v = wsm
            e = lambda *a, **k: stage_ops.append(S.emit(*a, **k))
            if part == "bias":
                return prep_bias(l, bsf, btmp, e)
            e(DVE, lambda h, o=wsm_v, i=wsf_v: h.tensor_tensor(
                out=o[:, :, :], in0=i, in1=tril[:, :].unsqueeze(1).broadcast_to([128, 4, 128]), op=ALU.mult),
                reads=[wsfb, buf("tril")], writes=[buf("wsm")])
            pt, ptb = bankT.alloc()

            def f_wsT(h, pt=pt, wsm_v=wsm_v):
                r = None
                for hh in range(4):
                    r = h.transpose(out=pt[:, hh * 128:(hh + 1) * 128], in_=wsm_v[:, hh, :], identity=ident[:, :])
                return r
            e(PE, f_wsT, reads=[buf("wsm"), buf("ident")], writes=[ptb])
            e(DVE, lambda h, pt=pt, l=l: h.tensor_copy(
                out=wsT[:, l, :, :].rearrange("p h i -> p (h i)"), in_=pt[:, 0:512]),
                reads=[ptb], writes=[buf(f"wsT{l}")])

        def prep_bias(l, bsf, btmp, e):
            bb = buf(f"bsf{l}")
            e(DVE, lambda h: h.tensor_copy(out=bhi[:, :, :], in_=bsf[:, :, :]), reads=[bb], writes=[buf("bhi")])
            e(DVE, lambda h: h.tensor_copy(out=bhif[:, :, :], in_=bhi[:, :, :]), reads=[buf("bhi")], writes=[buf("bhif")])
            e(DVE, lambda h: h.tensor_tensor(out=btmp[:, :, :], in0=bsf[:, :, :], in1=bhif[:, :, :], op=ALU.subtract),
              reads=[bb, buf("bhif")], writes=[bb])
            e(DVE, lambda h: h.tensor_copy(out=blo[:, :, :], in_=btmp[:, :, :]), reads=[bb], writes=[buf("blo")])
            e(DVE, lambda h: h.tensor_scalar(out=bhif[:, :, :], in0=bhi[:, :, :], scalar1=m01[:, 0:1], scalar2=None,
                                             op0=ALU.mult),
              reads=[buf("bhi"), buf("m01"), buf("blo")], writes=[buf("bhif")])
            e(DVE, lambda h, l=l: h.scalar_tensor_tensor(out=bias2[:, l, :, :], in0=blo[:, :, :], scalar=m01[:, 1:2],
                                                         in1=bhif[:, :, :], op0=ALU.mult, op1=ALU.add),
              reads=[buf("blo"), buf("bhif"), buf("m01")], writes=[buf(f"bias2{l}")])


        def xslot(G, i):
            return (G * TPG + i) % NSLOT

        def xbuf(G, i):
            return buf(f"x{xslot(G, i)}")

        def xap(G, i):
            return xg[:, xslot(G, i), :]

        folded = set()

        def ensure_fold(l, cb, only=False):
            order = list(WIN_ORDER)
            idx = order.index(cb)
            for c2 in order[idx:idx + (1 if only else 2)]:
                if (l, c2) not in folded:
                    folded.add((l, c2))
                    fold_g(l, c2)

        def rstd_ops(src_ap, src_bufs, n_feat, nparts=128, no_pool=False):
            c = newcol()
            cb_ = buf(f"ssx_{c}")
            S.emit(ACT, lambda h: h.activation(out=junk[0:nparts, 0:n_feat], in_=src_ap, func=AF.Square,
                                               accum_out=ssx[0:nparts, c:c + 1]),
                   reads=list(src_bufs) + [buf("ssx")], writes=[cb_, buf("vscr")])
            rb = buf(f"rsx_{c}")
            if no_pool:
                S.emit(ACT, lambda h: h.activation(out=rsx[0:nparts, c:c + 1], in_=ssx[0:nparts, c:c + 1], func=AF.Sqrt,
                                                   scale=1.0 / n_feat, bias=epsb[0:nparts, 0:1]),
                       reads=[cb_, buf("epsb")], writes=[rb])
                S.emit(DVE, lambda h: h.reciprocal(out=rsx[0:nparts, c:c + 1], in_=rsx[0:nparts, c:c + 1]),
                       reads=[rb], writes=[rb])
                return c, rb
            S.emit(DVE, lambda h: h.tensor_scalar(out=rsx[0:nparts, c:c + 1], in0=ssx[0:nparts, c:c + 1],
                                                  scalar1=1.0 / n_feat, scalar2=EPS, op0=ALU.mult, op1=ALU.add),
                   reads=[cb_], writes=[rb])
            S.emit(POOL, lambda h: h.tensor_tensor(out=rsx[0:nparts, c:c + 1], in0=rsx[0:nparts, c:c + 1],
                                                   in1=nhalf[0:nparts, 0:1], op=ALU.pow),
                   reads=[rb, buf("nhalf")], writes=[rb])
            return c, rb

        def norm_pre(G, l, i):
            xa = xap(G, i)
            c, rb = rstd_ops(xa, [xbuf(G, i)], D, no_pool=(G == 0 and l == 0))
            j = i % NHB
            S.emit(ACT, lambda h: h.activation(out=hb[:, j, :], in_=xa, func=AF.Copy, scale=rsx[:, c:c + 1]),
                   reads=[xbuf(G, i), rb], writes=[buf(f"hb{j}")])

        def norm_tr(G, l, i):
            j = i % NHB
            pt, ptb = bankT.alloc()

            def f_tr(h):
                r = None
                for k in range(8):
                    r = h.transpose(out=pt[:, k * 128:(k + 1) * 128], in_=hb[:, j, k * 128:(k + 1) * 128],
                                    identity=ident[:, :])
                return r
            S.emit(PE, f_tr, reads=[buf(f"hb{j}"), buf("ident")], writes=[ptb])
            S.emit(DVE, lambda h: h.tensor_copy(out=hT[:, :, i * 128:(i + 1) * 128],
                                                in_=pt[:, :].rearrange("p (k t) -> p k t", k=8)),
                   reads=[ptb], writes=[buf(f"hT{i}")])

        def hT_bufs():
            return [buf(f"hT{i}") for i in range(TPG)]

        def proj_fm(l, col0):
            cb = (col0 // 512) * 512
            ensure_fold(l, cb)
            pf, pfb = bankF.alloc()

            def f(h):
                r = None
                for k in range(8):
                    r = h.matmul(out=pf[:, 0:GT], lhsT=win[l][:, k, col0:col0 + 128], rhs=hT[:, k, :],
                                 start=(k == 0), stop=(k == 7))
                return r
            S.emit(PE, f, reads=hT_bufs() + [buf(f"win{l}_{cb}")], writes=[pfb])
            return pf, pfb

        def proj_tm(l, i, col0):
            ensure_fold(l, col0)
            pf, pfb = bankF.alloc()

            def f(h):
                r = None
                for k in range(8):
                    r = h.matmul(out=pf[:, :], lhsT=hT[:, k, i * 128:(i + 1) * 128], rhs=win[l][:, k, col0:col0 + 512],
                                 start=(k == 0), stop=(k == 7))
                return r
            S.emit(PE, f, reads=[buf(f"hT{i}"), buf(f"win{l}_{col0}")], writes=[pfb])
            return pf, pfb

        def pooling(G, l, g):
            w = WINDOWS[g]
            sl = g % 2
            ab = buf(f"a_sb{sl}")
            ahb = buf(f"a_halo{sl}")
            a = a_sb[:, sl, :]
            PENG = DVE if g >= 2 else POOL
            S.emit(PENG, lambda h: h.tensor_copy(out=a_sb[:, sl, 0:HALO], in_=halo[:, l, g, :]),
                   reads=[buf(f"halo{l}_{g}")], writes=[ahb])
            T0, T1 = Tt[:, 0, :], Tt[:, 1, :]
            tb = [buf("T0"), buf("T1")]
            steps = g + 1
            src, srcb = a, [ab, ahb]
            shift = 1
            c_from = HALO - (w - 1) + 1
            for s_ in range(steps):
                dst = (T0, T1)[s_ % 2]
                dstb = tb[s_ % 2]
                S.emit(PENG, lambda h, dst=dst, src=src, c0=c_from, sh=shift: h.tensor_tensor(
                    out=dst[:, c0:AW], in0=src[:, c0:AW], in1=src[:, c0 - sh:AW - sh], op=ALU.add),
                    reads=srcb, writes=[dstb])
                src, srcb = dst, [dstb]
                shift *= 2
                c_from += shift
            assert c_from - shift == HALO, (g, c_from, shift)
            S.emit(PENG, lambda h: h.tensor_copy(out=halo[:, l, g, :], in_=a_sb[:, sl, GT:AW]),
                   reads=[ab], writes=[buf(f"halo{l}_{g}")])
            S.emit(DVE, lambda h, src=src: h.scalar_tensor_tensor(
                out=d_sb[:, g, :], in0=src[:, HALO:AW], scalar=1.0 / w, in1=a_sb[:, sl, HALO:AW],
                op0=ALU.mult, op1=ALU.subtract),
                reads=srcb + [ab], writes=[buf(f"d{g}")])
            if G == 0:
                S.emit(DVE, lambda h, src=src: h.tensor_tensor(out=fix_t[:, :], in0=src[:, HALO:HALO + 16],
                                                               in1=invcnt[:, g, :], op=ALU.mult),
                       reads=srcb + [buf("invcnt")], writes=[buf("fix_t")])
                S.emit(DVE, lambda h: h.tensor_tensor(out=d_sb[:, g, 0:16], in0=fix_t[:, :], in1=a_sb[:, sl, HALO:HALO + 16],
                                                      op=ALU.subtract),
                       reads=[buf("fix_t"), ab], writes=[buf(f"d{g}")])

        def v_tile(G, l, i, pf, pfb, last):
            S.emit(ACT, lambda h: h.activation(out=vscr[:, :], in_=pf[:, :], func=AF.Square),
                   reads=[pfb], writes=[buf("vscr")])
            c = newcol4()
            sb_ = buf(f"ss4_{c}")
            S.emit(DVE, lambda h: h.tensor_reduce(out=ss4[:, c, :], in_=vscr[:, :].rearrange("p (h d) -> p h d", h=4),
                                                  axis=AX.X, op=ALU.add),
                   reads=[buf("vscr"), buf("ss4")], writes=[sb_])
            rb = buf(f"rs4_{c}")
            S.emit(DVE, lambda h: h.tensor_scalar(out=rs4[:, c, :], in0=ss4[:, c, :], scalar1=1.0 / 128, scalar2=EPS,
                                                  op0=ALU.mult, op1=ALU.add), reads=[sb_], writes=[rb])
            S.emit(POOL, lambda h: h.tensor_tensor(out=rs4[:, c, :], in0=rs4[:, c, :], in1=nhalf[:, :], op=ALU.pow),
                   reads=[rb, buf("nhalf")], writes=[rb])
            def f(h):
                r = None
                for hh in range(4):
                    o = vnf[:, hh * 128:(hh + 1) * 128] if last else vn_sb[:, i, hh * 128:(hh + 1) * 128]
                    r = h.scalar_tensor_tensor(out=o, in0=pf[:, hh * 128:(hh + 1) * 128], scalar=rs4[:, c, hh:hh + 1],
                                               in1=vg_bc[:, l, hh * 128:(hh + 1) * 128], op0=ALU.mult, op1=ALU.mult)
                return r
            def stage2():
                if not last:
                    S.emit(DVE, f, reads=[pfb, rb, buf(f"vg_bc{l}")], writes=[buf(f"vn{i}")])
                else:
                    S.emit(DVE, f, reads=[pfb, rb, buf(f"vg_bc{l}")], writes=[buf("vnf")])
                    S.emit(DVE, lambda h: h.tensor_copy(out=vn_sb[:, i, :], in_=vnf[:, :]), reads=[buf("vnf")],
                           writes=[buf(f"vn{i}")])
                    S.emit(SP, lambda h: h.dma_start(out=cvp_d[l], in_=vnf[:, :]), reads=[buf("vnf")], dma=_okey())
            return stage2

        vstate = {}

        def emit_v_proj(G, l, i):
            st = vstate.setdefault((G, l), {"pend": None, "q": []})
            pf, pfb = proj_tm(l, i, C_V)
            st["q"].append((i, pf, pfb))

        def emit_v_chain(G, l):
            st = vstate[(G, l)]
            for (i, pf, pfb) in st["q"]:
                st2 = v_tile(G, l, i, pf, pfb, last=(G == NG - 1 and i == TPG - 1))
                if st["pend"] is not None:
                    st["pend"]()
                st["pend"] = st2
            st["q"] = []

        def emit_v(G, l, i):
            emit_v_proj(G, l, i)
            emit_v_chain(G, l)

        def finish_v(G, l):
            st = vstate[(G, l)]
            emit_v_chain(G, l)
            if st["pend"] is not None:
                st["pend"]()
                st["pend"] = None

        def layer_group(G, l, before_wout=None, after_tile=None, mid_proj=None, after_proj=None):
            lastG = (G == NG - 1)
            if G == 0 and l == 0:
                prep_ws(0, "mask")
                prep_ws(1, "mask")
            if (G, l) not in vstate:
                for i in range(TPG):
                    emit_v(G, l, i)
            finish_v(G, l)
            if G == 0 and l == 0:
                for cb in WIN_ORDER:
                    ensure_fold(0, cb, only=True)
                prep_ws(0, "bias")
                prep_ws(1, "bias")
                for t in (4, 5, 7):
                    load_tile(t, deps=stage_ops + [ident_cast_op])
            if G == 0 and l == 1:
                for cb in WIN_ORDER:
                    ensure_fold(1, cb, only=True)
            for g in range(4):
                pf, pfb = proj_fm(l, C_AIN + g * 128)
                S.emit(ACT, lambda h, pf=pf, g=g: h.copy(out=a_sb[:, g % 2, HALO:AW], in_=pf[:, 0:GT]),
                       reads=[pfb], writes=[buf(f"a_sb{g % 2}")])
                pooling(G, l, g)
            for g in range(4):
                pf, pfb = proj_fm(l, C_AGATE + g * 128)
                S.emit(ACT, lambda h, pf=pf, g=g: h.activation(out=sg[:, g, :], in_=pf[:, 0:GT], func=AF.Silu),
                       reads=[pfb], writes=[buf(f"sg{g}")])
            if lastG:
                pf, pfb = proj_tm(l, TPG - 1, C_AIN)
                S.emit(ACT, lambda h, pf=pf: h.copy(out=pp_sb[96:128, :], in_=pf[96:128, :]), reads=[pfb],
                       writes=[buf("pp_sb")])
                S.emit(SP, lambda h: h.dma_start(out=pp_d[l], in_=pp_sb[128 - PB:128, :]), reads=[buf("pp_sb")],
                       dma=_okey())
            if mid_proj is not None:
                mid_proj()
            for g in range(4):
                pf, pfb = bankF.alloc()
                S.emit(PE, lambda h, pf=pf, g=g: h.matmul(out=pf[:, 0:GT], lhsT=wpool[:, l, g, :], rhs=d_sb[:, g, :],
                                                          start=True, stop=True),
                       reads=[buf(f"d{g}"), buf(f"wpool{l}")], writes=[pfb])
                S.emit(DVE, lambda h, pf=pf, g=g: h.scalar_tensor_tensor(
                    out=mix[:, g, :], in0=pf[:, 0:GT], scalar=psc[:, l, g, :], in1=sg[:, g, :],
                    op0=ALU.mult, op1=ALU.mult),
                    reads=[pfb, buf(f"psc{l}"), buf(f"sg{g}")], writes=[buf(f"mix{g}")])
                hh = g
                pf, pfb = proj_fm(l, C_BGATE + hh * 128)
                S.emit(ACT, lambda h, pf=pf, hh=hh: h.activation(out=sg[:, hh, :], in_=pf[:, 0:GT], func=AF.Silu),
                       reads=[pfb], writes=[buf(f"sg{hh}")])
            for hh in range(4):
                pf, pfb = proj_fm(l, C_U + hh * 128)
                S.emit(DVE, lambda h, pf=pf, hh=hh: h.tensor_tensor(out=sg[:, hh, :], in0=pf[:, 0:GT], in1=sg[:, hh, :],
                                                                    op=ALU.mult),
                       reads=[pfb, buf(f"sg{hh}")], writes=[buf(f"sg{hh}")])
            if after_proj is not None:
                after_proj()
            for hh in range(4):
                pf, pfb = bankF.alloc()

                def f(h, pf=pf, hh=hh):
                    r = None
                    h.matmul(out=pf[:, 0:GT].rearrange("p (c i) -> p c i", c=TPG), lhsT=ones2[:, :],
                             rhs=bias2[:, l, hh, :].unsqueeze(1).broadcast_to([2, TPG, 128]), start=True, stop=False)
                    for c in range(TPG):
                        r = h.matmul(out=pf[:, c * 128:(c + 1) * 128], lhsT=vn_sb[:, c, hh * 128:(hh + 1) * 128],
                                     rhs=wsT[:, l, hh, :], start=False, stop=(c == TPG - 1))
                    return r
                S.emit(PE, f, reads=[buf(f"vn{c}") for c in range(TPG)] + [buf(f"wsT{l}"), buf(f"bias2{l}"), buf("ones2")],
                       writes=[pfb])
                S.emit(DVE, lambda h, pf=pf, hh=hh: h.tensor_tensor(out=mix[:, 4 + hh, :], in0=pf[:, 0:GT],
                                                                    in1=sg[:, hh, :], op=ALU.mult),
                       reads=[pfb, buf(f"sg{hh}")], writes=[buf(f"mix{4 + hh}")])
            if before_wout is not None:
                before_wout()
            for i in range(TPG):
                for hf in range(2):
                    pf, pfb = bankF.alloc()

                    def f(h, pf=pf, i=i, hf=hf):
                        r = None
                        for k in range(8):
                            r = h.matmul(out=pf[:, :], lhsT=mix[:, k, i * 128:(i + 1) * 128],
                                         rhs=wout[l][:, k, hf * 512:(hf + 1) * 512], start=(k == 0), stop=(k == 7))
                        return r
                    S.emit(PE, f, reads=[buf(f"mix{k}") for k in range(8)] + [buf(f"wout{l}_{hf}")], writes=[pfb])
                    xa = xap(G, i)
                    S.emit(DVE, lambda h, pf=pf, xa=xa, hf=hf: h.tensor_tensor(
                        out=xa[:, hf * 512:(hf + 1) * 512], in0=pf[:, :], in1=xa[:, hf * 512:(hf + 1) * 512], op=ALU.add),
                        reads=[pfb, xbuf(G, i)], writes=[xbuf(G, i)])
                if after_tile is not None:
                    after_tile(i)

        def final_tile(G, i):
            xa = xap(G, i)
            c, rb = rstd_ops(xa, [xbuf(G, i)], D)
            S.emit(DVE, lambda h: h.scalar_tensor_tensor(out=xa, in0=xa, scalar=rsx[:, c:c + 1], in1=g_bc[:, :],
                                                         op0=ALU.mult, op1=ALU.mult),
                   reads=[xbuf(G, i), rb, buf("g_bc2")], writes=[xbuf(G, i)])
            t = G * TPG + i
            slot = t % NSLOT
            S.emit(SP, lambda h, s=slot, r=t * 128: h.dma_start(out=y_d[r:r + 128, :], in_=xg[:, s, :]),
                   reads=[buf(f"x{slot}")], dma=f"yo{slot}")
            if t + NSLOT < NT:
                load_tile(t + NSLOT)

        xsb = buf("xs")

        def sample_dmas(l):
            for t in range(2):
                S.emit(SP, lambda h, l=l, t=t: h.dma_start(
                    out=st_sb[:, t, :], in_=st_d[l][8 * t:8 * t + 8].rearrange("b r c -> (b r) c")),
                    writes=[buf(f"st_sb{t}")], dma=f"st{l}_{t}")

        def sample_inputs():
            S.emit(SP, lambda h: h.dma_start(out=xs_sb[:, :], in_=xs_d[:, :]), writes=[xsb], dma="xsl")
            sample_dmas(0)
            for l in range(2):
                S.emit(SP, lambda h, l=l: h.dma_start(out=pso_d[l][:, 0:PB - 1, :], in_=st_d[l][:, 1:PB, :]), dma=_okey())

        def transpose_s(src, srcb, nk, dst, dstb):
            pt, ptb = bankT.alloc()

            def f(h):
                r = None
                for k in range(nk):
                    r = h.transpose(out=pt[:, k * NS:(k + 1) * NS], in_=src[:, k * 128:(k + 1) * 128],
                                    identity=ident[0:NS, 0:NS])
                return r
            S.emit(PE, f, reads=[srcb, buf("ident")], writes=[ptb])
            S.emit(DVE, lambda h: h.tensor_copy(out=dst, in_=pt[:, 0:nk * NS].rearrange("p (k t) -> p k t", k=nk)),
                   reads=[ptb], writes=[dstb])

        def proj_s(l, col0):
            pf, pfb = bankF.alloc()

            def f(h):
                r = None
                for k in range(8):
                    r = h.matmul(out=pf[0:NS, :], lhsT=hTs[:, k, :], rhs=win[l][:, k, col0:col0 + 512],
                                 start=(k == 0), stop=(k == 7))
                return r
            S.emit(PE, f, reads=[buf("hTs"), buf(f"win{l}_{col0}")], writes=[pfb])
            return pf, pfb

        def sample_front(l, part="all"):
            if part in ("all", "chain"):
                c, rb = rstd_ops(xs_sb[:, :], [xsb], D, nparts=NS)
                S.emit(DVE, lambda h, c=c, l=l: h.tensor_scalar(out=hbs[:, :], in0=xs_sb[:, :], scalar1=rsx[0:NS, c:c + 1],
                                                                scalar2=None, op0=ALU.mult),
                       reads=[xsb, rb], writes=[buf("hbs")])
            if part in ("all", "tr"):
                transpose_s(hbs, buf("hbs"), 8, hTs[:, :, :], buf("hTs"))

        def sample_rest(l, mid_hook=None):
            sample_rest_A(l)
            sample_rest_B(l)
            sample_rest_C(l, mid_hook)

        def sample_rest_A(l):
            S.emit(SP, lambda h, l=l: h.dma_start(out=psc_bc[:, :], in_=psc_d[l:l + 1, :].broadcast_to([NS, WA])),
                   writes=[buf("psc_bc")], dma=f"pscbc{l}")
            pa, pab = proj_s(l, C_AIN)
            pS, pSb = bankF.alloc()

            def f_sel(h, pS=pS):
                r = None
                for g in range(4):
                    for t in range(2):
                        r = h.matmul(out=pS[0:NS, g * 128:(g + 1) * 128], lhsT=sel[:, t, g, :],
                                     rhs=st_sb[:, t, g * 128:(g + 1) * 128], start=(t == 0), stop=(t == 1))
                return r
            S.emit(PE, f_sel, reads=[buf("st_sb0"), buf("st_sb1"), buf("sel")], writes=[pSb])
            if l == 0:
                sample_dmas(1)
            pv, pvb = proj_s(l, C_V)
            pg, pgb = proj_s(l, C_AGATE)
            pbg, pbgb = proj_s(l, C_BGATE)
            pu, pub = proj_s(l, C_U)
            S.emit(ACT, lambda h, pf=pa: h.copy(out=ains[:, :], in_=pf[0:NS, :]), reads=[pab], writes=[buf("ains")])
            S.emit(SP, lambda h, l=l: h.dma_start(out=pso_d[l][:, PB - 1, :], in_=ains[:, :]), reads=[buf("ains")],
                   dma=_okey())
            for g in range(4):
                w = WINDOWS[g]
                S.emit(DVE, lambda h, g=g, w=w: h.tensor_scalar(out=tmp_s[:, g * 128:(g + 1) * 128],
                                                                in0=ains[:, g * 128:(g + 1) * 128], scalar1=(1.0 / w - 1.0),
                                                                scalar2=None, op0=ALU.mult),
                       reads=[buf("ains")], writes=[buf("tmp_s")])
                S.emit(DVE, lambda h, g=g, w=w, pS=pS: h.scalar_tensor_tensor(
                    out=ds_s[:, g * 128:(g + 1) * 128], in0=pS[0:NS, g * 128:(g + 1) * 128], scalar=1.0 / w,
                    in1=tmp_s[:, g * 128:(g + 1) * 128], op0=ALU.mult, op1=ALU.add),
                    reads=[pSb, buf("tmp_s")], writes=[buf("ds_s")])
            S.emit(ACT, lambda h, pf=pv: h.activation(out=s_s[:, :], in_=pf[0:NS, :], func=AF.Square), reads=[pvb],
                   writes=[buf("s_s")])
            c4 = newcol4()
            sb_ = buf(f"ss4_{c4}")
            S.emit(DVE, lambda h, c4=c4: h.tensor_reduce(out=ss4[0:NS, c4, :],
                                                         in_=s_s[:, :].rearrange("p (h d) -> p h d", h=4),
                                                         axis=AX.X, op=ALU.add),
                   reads=[buf("s_s"), buf("ss4")], writes=[sb_])
            rb4 = buf(f"rs4_{c4}")
            S.emit(DVE, lambda h, c4=c4: h.tensor_scalar(out=rs4[0:NS, c4, :], in0=ss4[0:NS, c4, :], scalar1=1.0 / 128,
                                                         scalar2=EPS, op0=ALU.mult, op1=ALU.add), reads=[sb_], writes=[rb4])
            S.emit(POOL, lambda h, c4=c4: h.tensor_tensor(out=rs4[0:NS, c4, :], in0=rs4[0:NS, c4, :], in1=nhalf[0:NS, :],
                                                          op=ALU.pow), reads=[rb4, buf("nhalf")], writes=[rb4])
            S.emit(ACT, lambda h, pf=pg: h.activation(out=sga_s[:, :], in_=pf[0:NS, :], func=AF.Silu), reads=[pgb],
                   writes=[buf("sga_s")])
            S.emit(ACT, lambda h, pf=pbg: h.activation(out=sgb_s[:, :], in_=pf[0:NS, :], func=AF.Silu), reads=[pbgb],
                   writes=[buf("sgb_s")])
            def f_vn(h, pf=pv, c4=c4, l=l):
                r = None
                for hh in range(4):
                    r = h.scalar_tensor_tensor(out=vns[:, hh * 128:(hh + 1) * 128], in0=pf[0:NS, hh * 128:(hh + 1) * 128],
                                               scalar=rs4[0:NS, c4, hh:hh + 1], in1=vg_bc[0:NS, l, hh * 128:(hh + 1) * 128],
                                               op0=ALU.mult, op1=ALU.mult)
                return r
            S.emit(DVE, f_vn, reads=[pvb, rb4, buf(f"vg_bc{l}")], writes=[buf("vns")])
            S.emit(SP, lambda h, l=l: h.dma_start(out=cvs_d[l], in_=vns[:, :]), reads=[buf("vns")], dma=_okey())

            def f_s(h, l=l):
                r = None
                for hh in range(4):
                    r = h.tensor_scalar(out=s_s[:, hh * 128:(hh + 1) * 128], in0=vns[:, hh * 128:(hh + 1) * 128],
                                        scalar1=ws00[:, l, hh, :], scalar2=bs0[:, l, hh, :], op0=ALU.mult, op1=ALU.add)
                return r
            S.emit(DVE, f_s, reads=[buf("vns"), buf(f"ws00{l}"), buf(f"bs0{l}")], writes=[buf("s_s")])
            transpose_s(ds_s, buf("ds_s"), 4, dsT[:, :, :], buf("dsT"))
            S.emit(DVE, lambda h, pf=pu: h.tensor_tensor(out=sgb_s[:, :], in0=pf[0:NS, :], in1=sgb_s[:, :], op=ALU.mult),
                   reads=[pub, buf("sgb_s")], writes=[buf("sgb_s")])
            S.emit(DVE, lambda h: h.tensor_tensor(out=mixs[:, WA:D], in0=sgb_s[:, :], in1=s_s[:, :], op=ALU.mult),
                   reads=[buf("sgb_s"), buf("s_s")], writes=[buf("mixs")])

        def sample_rest_B(l):
            pP, pPb = bankF.alloc()

            def f_pl(h, pP=pP, l=l):
                r = None
                for g in range(4):
                    r = h.matmul(out=pP[0:NS, g * 128:(g + 1) * 128], lhsT=dsT[:, g, :], rhs=wpool[:, l, g, :],
                                 start=True, stop=True)
                return r
            S.emit(PE, f_pl, reads=[buf("dsT"), buf(f"wpool{l}")], writes=[pPb])
            S.emit(DVE, lambda h, pP=pP, l=l: h.tensor_tensor(out=tmp_s[:, :], in0=pP[0:NS, :], in1=psc_bc[:, :],
                                                              op=ALU.mult),
                   reads=[pPb, buf("psc_bc")], writes=[buf("tmp_s")])
            S.emit(DVE, lambda h: h.tensor_tensor(out=mixs[:, 0:WA], in0=tmp_s[:, :], in1=sga_s[:, :], op=ALU.mult),
                   reads=[buf("tmp_s"), buf("sga_s")], writes=[buf("mixs")])

        def sample_rest_C(l, mid_hook=None):
            transpose_s(mixs, buf("mixs"), 8, mixTs[:, :, :], buf("mixTs"))
            if mid_hook is not None:
                mid_hook()
            for hf in range(2):
                pf, pfb = bankF.alloc()

                def f_o(h, pf=pf, hf=hf, l=l):
                    r = None
                    for k in range(8):
                        r = h.matmul(out=pf[0:NS, :], lhsT=mixTs[:, k, :], rhs=wout[l][:, k, hf * 512:(hf + 1) * 512],
                                     start=(k == 0), stop=(k == 7))
                    return r
                S.emit(PE, f_o, reads=[buf("mixTs"), buf(f"wout{l}_{hf}")], writes=[pfb])
                S.emit(DVE, lambda h, pf=pf, hf=hf: h.tensor_tensor(out=xs_sb[:, hf * 512:(hf + 1) * 512], in0=pf[0:NS, :],
                                                                    in1=xs_sb[:, hf * 512:(hf + 1) * 512], op=ALU.add),
                       reads=[pfb, xsb], writes=[xsb])
        def sample_final():
            c, rb = rstd_ops(xs_sb[:, :], [xsb], D, nparts=NS)
            S.emit(DVE, lambda h: h.scalar_tensor_tensor(out=xs_sb[:, :], in0=xs_sb[:, :], scalar=rsx[0:NS, c:c + 1],
                                                         in1=g_bc[0:NS, :], op0=ALU.mult, op1=ALU.mult),
                   reads=[xsb, rb, buf("g_bc2")], writes=[xsb])
            S.emit(SP, lambda h: h.dma_start(out=ys_d[:, :], in_=xs_sb[:, :]), reads=[xsb], dma=_okey())

        def finals_batched(G, load_next=False, split=False):
            cols = []
            for i in range(TPG):
                c = newcol()
                cb_ = buf(f"ssx_{c}")
                xa = xap(G, i)
                S.emit(ACT, lambda h, xa=xa, c=c: h.activation(out=junk[:, 0:D], in_=xa, func=AF.Square,
                                                               accum_out=ssx[:, c:c + 1]),
                       reads=[xbuf(G, i), buf("ssx")], writes=[cb_, buf("vscr")])
                cols.append((c, cb_, buf(f"rsx_{c}")))
            for i, (c, cb_, rb) in enumerate(cols):
                S.emit(DVE, lambda h, c=c: h.tensor_scalar(out=rsx[:, c:c + 1], in0=ssx[:, c:c + 1], scalar1=1.0 / D,
                                                           scalar2=EPS, op0=ALU.mult, op1=ALU.add), reads=[cb_], writes=[rb])
            for i, (c, cb_, rb) in enumerate(cols):
                S.emit(POOL, lambda h, c=c: h.tensor_tensor(out=rsx[:, c:c + 1], in0=rsx[:, c:c + 1], in1=nhalf[:, 0:1],
                                                            op=ALU.pow), reads=[rb, buf("nhalf")], writes=[rb])
            def stage_b(only=None):
                for i, (c, cb_, rb) in enumerate(cols):
                    if only is not None and i != only:
                        continue
                    xa = xap(G, i)
                    S.emit(DVE, lambda h, xa=xa, c=c: h.scalar_tensor_tensor(out=xa, in0=xa, scalar=rsx[:, c:c + 1],
                                                                             in1=g_bc[:, :], op0=ALU.mult, op1=ALU.mult),
                           reads=[xbuf(G, i), rb, buf("g_bc2")], writes=[xbuf(G, i)])
                    t = G * TPG + i
                    slot = t % NSLOT
                    S.emit(SP, lambda h, s_=slot, r=t * 128: h.dma_start(out=y_d[r:r + 128, :], in_=xg[:, s_, :]),
                           reads=[buf(f"x{slot}")], dma=f"yo{slot}")
                    if load_next and t + NSLOT < NT:
                        load_tile(t + NSLOT)
            if split:
                return stage_b
            stage_b()

        load_tile(6)
        for i in range(NHB):
            norm_pre(0, 0, i)
        for i in range(TPG):
            norm_tr(0, 0, i)
            if i + NHB < TPG:
                norm_pre(0, 0, i + NHB)
        for G in range(NG):
            def after0(i, G=G):
                norm_pre(G, 1, i)
                if i >= 1:
                    norm_tr(G, 1, i - 1)
                if i == TPG - 1:
                    for j in range(TPG - 1):
                        emit_v_proj(G, 1, j)
                    norm_tr(G, 1, i)
                    emit_v_proj(G, 1, i)
                    emit_v_chain(G, 1)
            fin_prev = {}

            def bw0(G=G, fin_prev=fin_prev):
                fin_prev["b"] = finals_batched(G - 1, load_next=True, split=True)

            def after0w(i, G=G, fin_prev=fin_prev):
                after0(i)
                if "b" in fin_prev:
                    fin_prev["b"](only=i)
            layer_group(G, 0, after_tile=after0w, before_wout=(bw0 if G > 0 else None))

            nxt = G + 1 < NG

            def after1(i, G=G, nxt=nxt):
                if not nxt:
                    return
                norm_tr(G + 1, 0, i)
                if i == 0:
                    norm_pre(G + 1, 0, 3)

            def aft1(G=G):
                norm_pre(G + 1, 0, 0)
                norm_pre(G + 1, 0, 1)
                norm_pre(G + 1, 0, 2)
            if not nxt:
                sample_inputs()

            def mid_last():
                sample_front(0, "chain")

            def bw_last():
                sample_front(0, "tr")
                sample_rest_A(0)

            def at_last(i):
                if i == 0:
                    sample_rest_B(0)
                elif i == 1:
                    sample_rest_C(0)
                    sample_front(1, "chain")
                elif i == 2:
                    sample_front(1, "tr")
            layer_group(G, 1, after_tile=(after1 if nxt else at_last), before_wout=(aft1 if nxt else bw_last),
                        mid_proj=(None if nxt else mid_last))

        fin_b = finals_batched(NG - 1, split=True)
        sample_rest(1, mid_hook=fin_b)
        sample_final()

        fin_deps = [op for op in S.ops[SP] if op.dma is not None and (op.dma.startswith("out") or op.dma.startswith("yo"))]
        S.emit(SP, lambda h: None, deps=fin_deps)

        keys = S.resolve()
        sems = {k: es.enter_context(nc.semaphore(f"s_{k}")) for k in keys}
        with nc.allow_non_contiguous_dma(reason="small strided parameter / state loads"), nc.Block() as block:
            @block.sync
            def _(h):
                run_sync(S, h, sems)

            @block.tensor
            def _(h):
                S.run_engine(PE, h, sems)

            @block.scalar
            def _(h):
                S.run_engine(ACT, h, sems)

            @block.vector
            def _(h):
                S.run_engine(DVE, h, sems)

            @block.gpsimd
            def _(h):
                S.run_engine(POOL, h, sems)
    return nc


def run_sync(S, h, sems):
    waited = {}
    ops = S.ops[SP]
    for n, op in enumerate(ops):
        for d in op.deps:
            key, val = d.tok
            if waited.get(key, 0) < val:
                h.wait_ge(sems[key], val)
                waited[key] = val
        if n == len(ops) - 1:
            break
        ins = op.fn(h)
        if op.dma is not None:
            ins.then_inc(sems[op.dma], 16)


_NC_CACHE = {}


def _consts():
    ident = np.eye(128, dtype=np.float32)
    tril = np.tril(np.ones((128, 128), dtype=np.float32))
    inv = np.zeros((4, 16), dtype=np.float32)
    for g, w in enumerate(WINDOWS):
        for t in range(16):
            inv[g, t] = 1.0 / min(t + 1, w)
    sel = np.zeros((120, 2, 4, NS), dtype=np.float32)
    for t in range(2):
        for bb in range(8):
            for r in range(PB):
                for g, w in enumerate(WINDOWS):
                    if r >= 16 - w:
                        sel[bb * PB + r, t, g, 8 * t + bb] = 1.0
    m01 = np.array([[1.0, 0.0], [0.0, 1.0]], dtype=np.float32)
    return dict(c_ident=ident, c_tril=tril, c_invcnt=inv.reshape(1, 64), c_sel=sel.reshape(120, 128), c_m01=m01)


def kernel(x_prompt, x_sample, state_pool, norm_g, w_in, w_pool, pool_scale, v_norm_g, w_s, b_s, w_out,
           final_norm_g):
    f = lambda a: np.ascontiguousarray(np.asarray(a, dtype=np.float32))
    x_prompt, x_sample, state_pool = f(x_prompt), f(x_sample), f(state_pool)
    shared = dict(norm_g=f(norm_g), w_in=f(w_in), w_pool=f(w_pool), pool_scale=f(pool_scale), v_norm_g=f(v_norm_g),
                  w_s=f(w_s), b_s=f(b_s), w_out=f(w_out), fng=f(final_norm_g).reshape(1, D))
    shared["gT_h"] = np.ascontiguousarray(shared["norm_g"].reshape(2, 8, 128).transpose(2, 0, 1).reshape(128, 16))
    shared["pscT_h"] = np.ascontiguousarray(shared["pool_scale"].reshape(2, 4, 128).transpose(2, 0, 1).reshape(128, 8))
    shared["ws00_h"] = np.ascontiguousarray(shared["w_s"][:, :, 0, 0].reshape(1, 8))
    shared["bs0_h"] = np.ascontiguousarray(shared["b_s"][:, :, 0].reshape(1, 8))
    shared.update(_consts())
    if "nc" not in _NC_CACHE:
        _NC_CACHE["nc"] = build_program()
    nc = _NC_CACHE["nc"]
    in_maps = []
    for c in range(NCORES):
        m = dict(shared)
        m["x"] = np.ascontiguousarray(x_prompt[c])
        m["xs"] = np.ascontiguousarray(x_sample[c * NS:(c + 1) * NS, 0, :])
        m["st"] = np.ascontiguousarray(state_pool[:, c * NS:(c + 1) * NS])
        in_maps.append(m)
    res = run_bass_kernel_spmd(nc, in_maps, core_ids=list(range(NCORES)))
    R = res.results
    y_prompt = np.stack([R[c]["y"] for c in range(NCORES)], axis=0)
    y_sample = np.concatenate([R[c]["ys"] for c in range(NCORES)], axis=0)[:, None, :]
    pool_prompt = np.stack([R[c]["pp"] for c in range(NCORES)], axis=1)
    pool_sample = np.concatenate([R[c]["pso"] for c in range(NCORES)], axis=1)
    cv_prompt = np.stack([R[c]["cvp"] for c in range(NCORES)], axis=1)
    cv_sample = np.concatenate([R[c]["cvs"] for c in range(NCORES)], axis=1)[:, :, None, :]
    return (y_prompt.astype(np.float32), y_sample.astype(np.float32), pool_prompt.astype(np.float32),
            pool_sample.astype(np.float32), cv_prompt.astype(np.float32), cv_sample.astype(np.float32))
```
